# Optimizing a Trainium2 kernel written in Bass

```python
import math
import jax
import jax.numpy as jnp
from jax import lax
import numpy as np

D_MODEL = 2048
BATCH = 4
SEQ = 2048
DEPTH = 4

GRID_W = 64
CTX_LEN = 256
HEAD = 64
RW_W = D_MODEL // 4
RW_H = RW_W // HEAD
LORA_W = 64
LORA_A = 64
LORA_G = 128
W_DECAY_SCALE = 0.606531
RW_LN_EPS = 64e-5
NA_W = D_MODEL // 4
NA_H = NA_W // HEAD
NA_KH = 8
NA_KW = 16
MB_INNER = D_MODEL // 2
MB_H = MB_INNER // HEAD
MB_G = 4
MB_R = MB_H // MB_G
MB_N = 128
MB_CONV = 5
MB_CHUNK = 128
ROPE_BASE = 10000.0
MIX = RW_W + NA_W + MB_INNER
PEER_HEADS = 8
D_KEY = 256
N_KEYS = 128
N_EXPERTS = N_KEYS * N_KEYS
PEER_TOPK = 16
PEER_BLOCK = 64
NORM_EPS = 1e-6
NEG_INF = -1e30
RW_COLS = 3 * RW_W + 2 * LORA_W + 2 * LORA_A + LORA_G
NA_COLS = 3 * NA_W
MB_CONV_CH = MB_INNER + 2 * MB_G * MB_N
MB_COLS = MB_INNER + MB_CONV_CH + 2 * MB_H
IN_COLS = RW_COLS + NA_COLS + MB_COLS

kernel_name = 'hybrid_rwkv7_natten_mamba2_peer_dit'


def split_cols(z, sizes):
    cuts = [int(s) for s in np.cumsum(sizes)[:-1]]
    return jnp.split(z, cuts, axis=-1)


def rms_norm(x, w):
    xf = x.astype(jnp.float32)
    y = xf * lax.rsqrt(jnp.mean(xf * xf, axis=-1, keepdims=True) + NORM_EPS)
    return (y * w.astype(jnp.float32)).astype(x.dtype)


def modulate(h, shift, scale):
    return h * (1 + scale) + shift


def centred_shift(z, mu):
    prev = jnp.pad(z[:, :-1], ((0, 0), (1, 0), (0, 0)))
    nxt = jnp.pad(z[:, 1:], ((0, 0), (0, 1), (0, 0)))
    return z + mu[0] * (prev - z) + mu[1] * (nxt - z)


def centred_dwconv(z, w, b):
    k = w.shape[0]
    p = k // 2
    t = z.shape[1]
    zp = jnp.pad(z, ((0, 0), (p, p), (0, 0)))
    out = zp[:, 0:t] * w[0]
    for i in range(1, k):
        out = out + zp[:, i:i + t] * w[i]
    return out + b


def axial_rope(z, row, col):
    n = z.shape[-1]
    half = n // 2
    nf = half // 2
    inv = ROPE_BASE ** (-jnp.arange(nf, dtype=jnp.float32) / nf)
    zf = z.astype(jnp.float32)

    def rot(u, p):
        ang = p.astype(jnp.float32)[:, None] * inv
        cos = jnp.cos(ang)[None, :, None, :]
        sin = jnp.sin(ang)[None, :, None, :]
        u1, u2 = u[..., :nf], u[..., nf:]
        return jnp.concatenate([u1 * cos - u2 * sin, u1 * sin + u2 * cos], axis=-1)

    return jnp.concatenate([rot(zf[..., :half], row), rot(zf[..., half:], col)], axis=-1).astype(z.dtype)


def rwkv7_streams(z, mu, w0, w2, a0, a2, g2, k_k, k_a):
    b, t, _ = z.shape
    z = centred_shift(z, mu)
    r, k, v, lw_f, lw_b, la_f, la_b, lg = split_cols(
        z, (RW_W, RW_W, RW_W, LORA_W, LORA_W, LORA_A, LORA_A, LORA_G))
    hd = lambda u: u.reshape(b, t, RW_H, HEAD)
    kk = hd(k * k_k).astype(jnp.float32)
    kk = (kk / jnp.maximum(jnp.linalg.norm(kk, axis=-1, keepdims=True), 1e-12)).astype(z.dtype)
    g = jax.nn.sigmoid(lg) @ g2
    dirs = []
    for d, (lw, la) in enumerate(((lw_f, la_f), (lw_b, la_b))):
        decay = jnp.exp(-W_DECAY_SCALE * jax.nn.sigmoid(w0[d] + jnp.tanh(lw) @ w2[d]))
        a = jax.nn.sigmoid(a0[d] + la @ a2[d])
        k_d = k * (1 + (a - 1) * k_a)
        dirs.append((hd(decay), hd(k_d), -kk, kk * hd(a)))
    return hd(r), hd(k), hd(v), g, dirs


def rwkv7_scan(r, v, dir_streams, s0, reverse):
    decay, k, a_vec, b_vec = dir_streams

    def step(state, inp):
        r_t, w_t, k_t, v_t, a_t, b_t = inp
        sa = jnp.einsum('bhij,bhj->bhi', state, a_t)
        state = (state * w_t[:, :, None, :] + sa[..., None] * b_t[:, :, None, :]
                 + v_t[..., None] * k_t[:, :, None, :])
        return state, jnp.einsum('bhij,bhj->bhi', state, r_t)

    xs = tuple(jnp.swapaxes(u, 0, 1) for u in (r, decay, k, v, a_vec, b_vec))
    s_fin, ys = lax.scan(step, s0, xs, reverse=reverse)
    return jnp.swapaxes(ys, 0, 1), s_fin


def rwkv7_out(y, r, k, v, g, r_k, ln_w, ln_b):
    b, t = y.shape[:2]
    yf = y.astype(jnp.float32)
    mean = jnp.mean(yf, axis=-1, keepdims=True)
    var = jnp.mean(jnp.square(yf - mean), axis=-1, keepdims=True)
    yn = ((yf - mean) * lax.rsqrt(var + RW_LN_EPS)).astype(y.dtype).reshape(b, t, RW_W) * ln_w + ln_b
    bonus = (jnp.sum(r * k * r_k, axis=-1, keepdims=True) * v).reshape(b, t, RW_W)
    return (yn + bonus) * g


def rwkv7_mixer(zc, zl, mu, w0, w2, a0, a2, g2, k_k, k_a, r_k, ln_w, ln_b):
    rc, kc, vc, gc, dc = rwkv7_streams(zc, mu, w0, w2, a0, a2, g2, k_k, k_a)
    rl, kl, vl, gl, dl = rwkv7_streams(zl, mu, w0, w2, a0, a2, g2, k_k, k_a)
    s0 = jnp.zeros((zl.shape[0], RW_H, HEAD, HEAD), zl.dtype)
    ys_c, ys_l = [], []
    for d, rev in enumerate((False, True)):
        y_c, s_ctx = rwkv7_scan(rc, vc, dc[d], s0, rev)
        y_l, _ = rwkv7_scan(rl, vl, dl[d], s_ctx, rev)
        ys_c.append(y_c)
        ys_l.append(y_l)
    return (rwkv7_out(ys_c[0] + ys_c[1], rc, kc, vc, gc, r_k, ln_w, ln_b),
            rwkv7_out(ys_l[0] + ys_l[1], rl, kl, vl, gl, r_k, ln_w, ln_b))


def natten_mixer(zc, zl, rpb):
    b, t, _ = zl.shape
    tc = zc.shape[1]
    qc, kc, vc = [u.reshape(b, tc, NA_H, HEAD) for u in split_cols(zc, (NA_W, NA_W, NA_W))]
    ql, kl, vl = [u.reshape(b, t, NA_H, HEAD) for u in split_cols(zl, (NA_W, NA_W, NA_W))]
    scale = HEAD ** -0.5
    s = jnp.einsum('bqhd,bkhd->bhqk', qc, kc).astype(jnp.float32) * scale
    y_c = jnp.einsum('bhqk,bkhd->bqhd', jax.nn.softmax(s, axis=-1).astype(vc.dtype), vc).reshape(b, tc, NA_W)
    rows = t // GRID_W
    kh = min(NA_KH, rows)
    r = jnp.arange(rows)
    key_rows = jnp.clip(r - kh // 2, 0, rows - kh)[:, None] + jnp.arange(kh)[None, :]
    cidx = jnp.arange(GRID_W)
    c0 = jnp.clip(cidx - NA_KW // 2, 0, GRID_W - NA_KW)
    in_win = (cidx[None, :] >= c0[:, None]) & (cidx[None, :] < c0[:, None] + NA_KW)
    qg = ql.reshape(b, rows, GRID_W, NA_H, HEAD)
    kg = kl.reshape(b, rows, GRID_W, NA_H, HEAD)[:, key_rows]
    vg = vl.reshape(b, rows, GRID_W, NA_H, HEAD)[:, key_rows]
    dr = key_rows - r[:, None]
    dc = cidx[None, :] - cidx[:, None]
    bias = rpb[:, (dr + NA_KH - 1)[:, None, :, None],
               jnp.clip(dc + NA_KW - 1, 0, 2 * NA_KW - 2)[None, :, None, :]]
    s_win = jnp.einsum('brqhd,brkjhd->bhrqkj', qg, kg).astype(jnp.float32) * scale + bias.astype(jnp.float32)[None]
    s_win = jnp.where(in_win[:, None, :], s_win, NEG_INF)
    s_ctx = jnp.einsum('brqhd,bchd->bhrqc', qg, kc).astype(jnp.float32) * scale
    nwin = kh * GRID_W
    p = jax.nn.softmax(jnp.concatenate([s_win.reshape(b, NA_H, rows, GRID_W, nwin), s_ctx], axis=-1),
                       axis=-1).astype(vl.dtype)
    p_win = p[..., :nwin].reshape(b, NA_H, rows, GRID_W, kh, GRID_W)
    y_l = (jnp.einsum('bhrqkj,brkjhd->brqhd', p_win, vg)
           + jnp.einsum('bhrqc,bchd->brqhd', p[..., nwin:], vc))
    return y_c, y_l.reshape(b, t, NA_W)


def segsum(a):
    n = a.shape[-1]
    ax = jnp.broadcast_to(a[..., :, None], a.shape + (n,))
    ax = jnp.where(jnp.tril(jnp.ones((n, n), bool), -1), ax, 0.0)
    cs = jnp.cumsum(ax, axis=-2)
    return jnp.where(jnp.tril(jnp.ones((n, n), bool)), cs, -jnp.inf)


def ssd_chunked(xh, dt, a_neg, bm, cm, s0):
    bsz, t, g, rr, p = xh.shape
    nc = t // MB_CHUNK
    dtype = xh.dtype
    dA = dt.astype(jnp.float32) * a_neg
    xc = (xh * dt[..., None].astype(dtype)).reshape(bsz, nc, MB_CHUNK, g, rr, p)
    bc = bm.reshape(bsz, nc, MB_CHUNK, g, MB_N)
    cc = cm.reshape(bsz, nc, MB_CHUNK, g, MB_N)
    dAc = jnp.transpose(dA.reshape(bsz, nc, MB_CHUNK, g, rr), (0, 3, 4, 1, 2))
    acs = jnp.cumsum(dAc, axis=-1)
    lmat = jnp.exp(segsum(dAc)).astype(dtype)
    cb = jnp.einsum('bclgn,bcsgn->bgcls', cc, bc)
    y_diag = jnp.einsum('bgrcls,bcsgrp->bclgrp', cb[:, :, None] * lmat, xc)
    decay_states = jnp.exp(acs[..., -1:] - acs).astype(dtype)
    states = jnp.einsum('bcsgn,bgrcs,bcsgrp->bcgrpn', bc, decay_states, xc)
    states = jnp.concatenate([s0[:, None], states], axis=1)
    chunk_decay = jnp.exp(segsum(jnp.pad(acs[..., -1], ((0, 0), (0, 0), (0, 0), (1, 0))))).astype(dtype)
    states = jnp.einsum('bgrzc,bcgrpn->bzgrpn', chunk_decay, states)
    y_off = jnp.einsum('bclgn,bcgrpn,bgrcl->bclgrp', cc, states[:, :-1], jnp.exp(acs).astype(dtype))
    return (y_diag + y_off).reshape(bsz, t, g, rr, p), states[:, -1]


def mamba2_mixer(zc, zl, conv_w, conv_b, dt_bias, a_log, d_skip, norm_w, row, col):
    def streams(z, rope):
        b, t, _ = z.shape
        gate, xbc, dt_f, dt_b = split_cols(z, (MB_INNER, MB_CONV_CH, MB_H, MB_H))
        xbc = jax.nn.silu(centred_dwconv(xbc, conv_w, conv_b))
        xs, bm, cm = split_cols(xbc, (MB_INNER, MB_G * MB_N, MB_G * MB_N))
        bm = bm.reshape(b, t, MB_G, MB_N)
        cm = cm.reshape(b, t, MB_G, MB_N)
        if rope:
            bm = axial_rope(bm, row, col)
            cm = axial_rope(cm, row, col)
        dts = [jax.nn.softplus(dt + dt_bias[d]).reshape(b, t, MB_G, MB_R) for d, dt in enumerate((dt_f, dt_b))]
        return gate, xs.reshape(b, t, MB_G, MB_R, HEAD), bm, cm, dts

    gate_c, x_c, b_c, c_c, dt_c = streams(zc, False)
    gate_l, x_l, b_l, c_l, dt_l = streams(zl, True)
    a_neg = -jnp.exp(a_log.astype(jnp.float32)).reshape(2, MB_G, MB_R)
    s0 = jnp.zeros((zl.shape[0], MB_G, MB_R, HEAD, MB_N), zl.dtype)
    ys_c, ys_l = [], []
    for d in range(2):
        f = (lambda u: jnp.flip(u, axis=1)) if d == 1 else (lambda u: u)
        yc_d, s_ctx = ssd_chunked(f(x_c), f(dt_c[d]), a_neg[d], f(b_c), f(c_c), s0)
        yl_d, _ = ssd_chunked(f(x_l), f(dt_l[d]), a_neg[d], f(b_l), f(c_l), s_ctx)
        ys_c.append(f(yc_d))
        ys_l.append(f(yl_d))

    def out(y, xh, gate):
        b, t = y.shape[:2]
        y = (y + xh * d_skip.reshape(MB_G, MB_R, 1)).reshape(b, t, MB_INNER) * jax.nn.silu(gate)
        yf = y.astype(jnp.float32).reshape(b, t, MB_G, MB_INNER // MB_G)
        yf = yf * lax.rsqrt(jnp.mean(yf * yf, axis=-1, keepdims=True) + NORM_EPS)
        return (yf.reshape(b, t, MB_INNER) * norm_w.astype(jnp.float32)).astype(y.dtype)

    return out(ys_c[0] + ys_c[1], x_c, gate_c), out(ys_l[0] + ys_l[1], x_l, gate_l)


def peer_ffn(h, wq, sub_keys, u_tab, v_tab):
    n, d = h.shape
    q = (h @ wq).reshape(n, PEER_HEADS, 2, D_KEY // 2)
    s = jnp.einsum('nhpd,hpkd->nhpk', q, sub_keys).astype(jnp.float32)
    s_top, i_top = lax.top_k(s, PEER_TOPK)
    cand = s_top[:, :, 0, :, None] + s_top[:, :, 1, None, :]
    best, pos = lax.top_k(cand.reshape(n, PEER_HEADS, PEER_TOPK * PEER_TOPK), PEER_TOPK)
    i1 = jnp.take_along_axis(i_top[:, :, 0], pos // PEER_TOPK, axis=-1)
    i2 = jnp.take_along_axis(i_top[:, :, 1], pos % PEER_TOPK, axis=-1)
    ids = i1 * N_KEYS + i2
    gates = jax.nn.softmax(best, axis=-1).astype(h.dtype)

    def apply(blk):
        hb, idb, gb = blk
        act = jax.nn.gelu(jnp.einsum('pd,phkd->phk', hb, u_tab[idb]), approximate=False)
        return jnp.einsum('phk,phkd->pd', gb * act, v_tab[idb])

    nb = n // PEER_BLOCK
    out = lax.map(apply, (h.reshape(nb, PEER_BLOCK, d),
                          ids.reshape(nb, PEER_BLOCK, PEER_HEADS, PEER_TOPK),
                          gates.reshape(nb, PEER_BLOCK, PEER_HEADS, PEER_TOPK)))
    return out.reshape(n, d)


def setup_inputs(seed: int = 0) -> dict:
    key = jax.random.key(seed)
    ks = iter(jax.random.split(key, 48))
    L, D = DEPTH, D_MODEL

    def nrm(shape, s):
        return jax.random.normal(next(ks), shape, jnp.float32) * s

    dt0 = jnp.exp(jax.random.uniform(next(ks), (L, 2, MB_H), jnp.float32,
                                     minval=math.log(1e-3), maxval=math.log(1e-1)))
    dt_bias = dt0 + jnp.log(-jnp.expm1(-dt0))
    a_log = jnp.log(jax.random.uniform(next(ks), (L, 2, MB_H), jnp.float32, minval=1.0, maxval=16.0))
    rw_mu = jax.random.uniform(next(ks), (L, 2, RW_COLS), jnp.float32, minval=0.0, maxval=0.4)
    return {
        'x': nrm((BATCH, SEQ, D), 1.0),
        'c': nrm((BATCH, D), 1.0),
        'ctx': nrm((BATCH, CTX_LEN, D), 1.0),
        'c_ctx': nrm((D,), 1.0),
        'ada_w': nrm((L, D, 6 * D), 0.5 * D ** -0.5),
        'ada_b': nrm((L, 6 * D), 0.02),
        'norm1_w': 1.0 + nrm((L, D), 0.02),
        'norm2_w': 1.0 + nrm((L, D), 0.02),
        'w_in': nrm((L, D, IN_COLS), D ** -0.5),
        'w_out': nrm((L, MIX, D), MIX ** -0.5),
        'rw_mu': rw_mu,
        'rw_w0': nrm((L, 2, RW_W), 1.0),
        'rw_w2': nrm((L, 2, LORA_W, RW_W), 0.5 * LORA_W ** -0.5),
        'rw_a0': nrm((L, 2, RW_W), 0.5),
        'rw_a2': nrm((L, 2, LORA_A, RW_W), LORA_A ** -0.5),
        'rw_g2': nrm((L, LORA_G, RW_W), LORA_G ** -0.5),
        'rw_kk': 0.85 + nrm((L, RW_W), 0.05),
        'rw_ka': 1.0 + nrm((L, RW_W), 0.05),
        'rw_rk': nrm((L, RW_H, HEAD), 0.1),
        'rw_ln_w': 1.0 + nrm((L, RW_W), 0.02),
        'rw_ln_b': nrm((L, RW_W), 0.01),
        'na_rpb': nrm((L, NA_H, 2 * NA_KH - 1, 2 * NA_KW - 1), 0.1),
        'mb_conv_w': nrm((L, MB_CONV, MB_CONV_CH), MB_CONV ** -0.5),
        'mb_conv_b': nrm((L, MB_CONV_CH), 0.02),
        'mb_dt_bias': dt_bias,
        'mb_a_log': a_log,
        'mb_d': 1.0 + nrm((L, MB_H), 0.1),
        'mb_norm_w': 1.0 + nrm((L, MB_INNER), 0.02),
        'pe_wq': nrm((L, D, PEER_HEADS * D_KEY), D ** -0.5),
        'pe_keys': nrm((L, PEER_HEADS, 2, N_KEYS, D_KEY // 2), (D_KEY // 2) ** -0.5),
        'pe_u': nrm((L, N_EXPERTS, D), D ** -0.5),
        'pe_v': nrm((L, N_EXPERTS, D), 0.1),
        'final_norm_w': 1.0 + nrm((D,), 0.02),
    }


def reference(x, c, ctx, c_ctx, ada_w, ada_b, norm1_w, norm2_w, w_in, w_out,
              rw_mu, rw_w0, rw_w2, rw_a0, rw_a2, rw_g2, rw_kk, rw_ka, rw_rk, rw_ln_w, rw_ln_b,
              na_rpb, mb_conv_w, mb_conv_b, mb_dt_bias, mb_a_log, mb_d, mb_norm_w,
              pe_wq, pe_keys, pe_u, pe_v, final_norm_w):
    b, t, d = x.shape
    tc = ctx.shape[1]
    pos = jnp.arange(t)
    row, col = pos // GRID_W, pos % GRID_W
    silu_c = jax.nn.silu(c)[:, None, :]
    silu_cc = jax.nn.silu(c_ctx)
    xl, xc = x, ctx
    for l in range(DEPTH):
        update_ctx = l < DEPTH - 1
        m_l = jnp.split(silu_c @ ada_w[l] + ada_b[l], 6, axis=-1)
        m_c = jnp.split(silu_cc @ ada_w[l] + ada_b[l], 6, axis=-1)
        zl = modulate(rms_norm(xl, norm1_w[l]), m_l[0], m_l[1]) @ w_in[l]
        zc = modulate(rms_norm(xc, norm1_w[l]), m_c[0], m_c[1]) @ w_in[l]
        rw_c, na_c, mb_c = split_cols(zc, (RW_COLS, NA_COLS, MB_COLS))
        rw_l, na_l, mb_l = split_cols(zl, (RW_COLS, NA_COLS, MB_COLS))
        y_rw = rwkv7_mixer(rw_c, rw_l, rw_mu[l], rw_w0[l], rw_w2[l], rw_a0[l], rw_a2[l], rw_g2[l],
                           rw_kk[l], rw_ka[l], rw_rk[l], rw_ln_w[l], rw_ln_b[l])
        y_na = natten_mixer(na_c, na_l, na_rpb[l])
        y_mb = mamba2_mixer(mb_c, mb_l, mb_conv_w[l], mb_conv_b[l], mb_dt_bias[l], mb_a_log[l],
                            mb_d[l], mb_norm_w[l], row, col)
        xl = xl + m_l[2] * (jnp.concatenate([y_rw[1], y_na[1], y_mb[1]], axis=-1) @ w_out[l])
        hl = modulate(rms_norm(xl, norm2_w[l]), m_l[3], m_l[4]).reshape(b * t, d)
        if update_ctx:
            xc = xc + m_c[2] * (jnp.concatenate([y_rw[0], y_na[0], y_mb[0]], axis=-1) @ w_out[l])
            hc = modulate(rms_norm(xc, norm2_w[l]), m_c[3], m_c[4]).reshape(b * tc, d)
            f = peer_ffn(jnp.concatenate([hc, hl], axis=0), pe_wq[l], pe_keys[l], pe_u[l], pe_v[l])
            xc = xc + m_c[5] * f[:b * tc].reshape(b, tc, d)
            xl = xl + m_l[5] * f[b * tc:].reshape(b, t, d)
        else:
            xl = xl + m_l[5] * peer_ffn(hl, pe_wq[l], pe_keys[l], pe_u[l], pe_v[l]).reshape(b, t, d)
    return rms_norm(xl, final_norm_w)
```

```python
import numpy as np
from contextlib import ExitStack
import concourse.bass as bass
import concourse.mybir as mybir
from concourse.bass_utils import run_bass_kernel_spmd

F32 = mybir.dt.float32
AF = mybir.ActivationFunctionType
ALU = mybir.AluOpType


class KB:
    ENG = ['pe', 'dve', 'act', 'pool', 'sp']

    def __init__(self):
        self.nc = bass.Bass("TRN2", target_bir_lowering=False)
        self.es = ExitStack()
        nc = self.nc
        self.gen = {e: 0 for e in self.ENG}
        self.semk = {e: e + "#0" for e in self.ENG}
        self.semh = {e + "#0": self.es.enter_context(nc.semaphore("s_" + e + "_0")) for e in self.ENG}
        self.cnt = {e: 0 for e in self.ENG}
        self.rot_limit = 40000
        self.ndma = 24
        self.dsem = [self.es.enter_context(nc.semaphore("d%d" % i)) for i in range(self.ndma)]
        self.dcnt = [0] * self.ndma
        self.dnext = 0
        self.waited = {e: {} for e in self.ENG}
        self.last_w = {}
        self.readers = {}
        self.prog = {e: [] for e in self.ENG}
        self.ninst = 0
        self.stack = [self.es]

    def sb(self, name, shape, dtype=F32):
        return self.stack[-1].enter_context(self.nc.sbuf_tensor(name, list(shape), dtype))

    def ps(self, name, shape, dtype=F32):
        return self.stack[-1].enter_context(self.nc.psum_tensor(name, list(shape), dtype))

    def push(self):
        self.stack.append(ExitStack())

    def pop(self):
        self.fence()
        self.stack.pop().close()

    def fence(self):
        need = {self.semk[e]: self.cnt[e] for e in self.ENG if self.cnt[e]}
        for i in range(self.ndma):
            if self.dcnt[i]:
                need[('d', i)] = self.dcnt[i]
        for e in self.ENG:
            self._emit_waits(e, dict(need), fence=True)
        for e in self.ENG:
            if self.cnt[e] > self.rot_limit:
                self.gen[e] += 1
                k = "%s#%d" % (e, self.gen[e])
                self.semk[e] = k
                self.semh[k] = self.es.enter_context(self.nc.semaphore("s_%s_%d" % (e, self.gen[e])))
                self.cnt[e] = 0
        self.last_w = {}
        self.readers = {}

    def dram(self, name, shape, dtype=F32, kind="Internal"):
        return self.nc.dram_tensor(name, list(shape), dtype, kind=kind)

    def _key(self, a):
        if isinstance(a, (str, tuple)):
            return a
        if hasattr(a, 'tensor'):
            return a.tensor.name
        return a.name

    def _semh(self, s):
        return self.semh[s] if isinstance(s, str) else self.dsem[s[1]]

    def _deps(self, reads, writes):
        need = {}

        def add(ev):
            if ev is None:
                return
            s, v = ev
            if need.get(s, 0) < v:
                need[s] = v
        for a in reads:
            add(self.last_w.get(self._key(a)))
        for a in writes:
            k = self._key(a)
            add(self.last_w.get(k))
            for s, v in self.readers.get(k, {}).items():
                add((s, v))
        return need

    def _emit_waits(self, e, need, fence=False):
        for s, v in need.items():
            if isinstance(s, str) and s.startswith('pe#') and e == 'pe' and not fence:
                continue
            if self.waited[e].get(s, 0) >= v:
                continue
            self.waited[e][s] = v
            semh = self._semh(s)
            self.prog[e].append(lambda eng, semh=semh, v=v: eng.wait_ge(semh, v))
            self.ninst += 1

    def _record(self, ev, reads, writes):
        wk = [self._key(a) for a in writes]
        for k in wk:
            self.last_w[k] = ev
            self.readers[k] = {}
        for a in reads:
            k = self._key(a)
            if k in wk:
                continue
            r = self.readers.setdefault(k, {})
            if r.get(ev[0], 0) < ev[1]:
                r[ev[0]] = ev[1]

    def op(self, e, fn, r=(), w=()):
        need = self._deps(r, w)
        self._emit_waits(e, need)
        self.cnt[e] += 1
        semh = self.semh[self.semk[e]]
        self.prog[e].append(lambda eng, fn=fn, semh=semh: fn(eng).then_inc(semh, 1))
        self.ninst += 1
        self._record((self.semk[e], self.cnt[e]), r, w)

    def dma(self, e, out, in_, r=None, w=None):
        r = [in_] if r is None else r
        w = [out] if w is None else w
        need = self._deps(r, w)
        i = self.dnext
        self.dnext = (self.dnext + 1) % self.ndma
        if self.dcnt[i]:
            need[('d', i)] = max(need.get(('d', i), 0), self.dcnt[i])
        self._emit_waits(e, need)
        self.dcnt[i] += 16
        semh = self.dsem[i]
        self.prog[e].append(lambda eng, out=out, in_=in_, semh=semh: eng.dma_start(out=out, in_=in_).then_inc(semh, 16))
        self.ninst += 1
        self._record((('d', i), self.dcnt[i]), r, w)

    def mm(self, out, lhsT, rhs, start=True, stop=True, r=None, w=None, fast=False):
        rr = [lhsT, rhs] if r is None else r
        if fast:
            lhsT = lhsT.bitcast(mybir.dt.float32r)
            rhs = rhs.bitcast(mybir.dt.float32r)
        self.op('pe', lambda eng: eng.matmul(out, lhsT, rhs, start=start, stop=stop),
                r=rr, w=[out] if w is None else w)

    def transpose(self, out, in_, ident):
        self.op('pe', lambda eng: eng.transpose(out, in_, ident), r=[in_, ident], w=[out])

    def act(self, out, in_, func, bias=None, scale=None, r=None, w=None, e='act'):
        kw = {}
        rr = [in_]
        if bias is not None:
            kw['bias'] = bias
            if not isinstance(bias, (int, float)):
                rr.append(bias)
        if scale is not None:
            kw['scale'] = scale
            if not isinstance(scale, (int, float)):
                rr.append(scale)
        self.op(e, lambda eng: eng.activation(out, in_, func, **kw), r=rr if r is None else r, w=[out] if w is None else w)

    def tt(self, out, in0, in1, op, e='dve', r=None, w=None):
        self.op(e, lambda eng: eng.tensor_tensor(out, in0, in1, op), r=[in0, in1] if r is None else r, w=[out] if w is None else w)

    def ts(self, out, in0, s1, op0, s2=None, op1=None, e='dve', r=None, w=None):
        rr = [in0]
        for s in (s1, s2):
            if s is not None and not isinstance(s, (int, float)):
                rr.append(s)
        if op1 is None:
            f = lambda eng: eng.tensor_scalar(out, in0, s1, None, op0)
        else:
            f = lambda eng: eng.tensor_scalar(out, in0, s1, s2, op0, op1)
        self.op(e, f, r=rr if r is None else r, w=[out] if w is None else w)

    def stt(self, out, in0, scalar, in1, op0, op1, r=None, w=None):
        rr = [in0, in1]
        if not isinstance(scalar, (int, float)):
            rr.append(scalar)
        self.op('dve', lambda eng: eng.scalar_tensor_tensor(out, in0, scalar, in1, op0, op1),
                r=rr if r is None else r, w=[out] if w is None else w)

    def copy(self, out, in_, e='dve', r=None, w=None):
        if e == 'act':
            f = lambda eng: eng.copy(out, in_)
        else:
            f = lambda eng: eng.tensor_copy(out, in_)
        self.op(e, f, r=[in_] if r is None else r, w=[out] if w is None else w)

    def memset(self, out, val, e='pool'):
        self.op(e, lambda eng: eng.memset(out, val), r=[], w=[out])

    def recip(self, out, in_):
        self.op('dve', lambda eng: eng.reciprocal(out, in_), r=[in_], w=[out])

    def build(self):
        for i in range(self.ndma):
            if self.dcnt[i]:
                semh = self.dsem[i]
                v = self.dcnt[i]
                self.prog['sp'].append(lambda eng, semh=semh, v=v: eng.wait_ge(semh, v))
        nc = self.nc
        with nc.Block() as block:
            for e, reg in (('sp', block.sync), ('pe', block.tensor), ('dve', block.vector),
                           ('act', block.scalar), ('pool', block.gpsimd)):
                prog = self.prog[e]

                def f(eng, prog=prog):
                    for p in prog:
                        p(eng)
                reg(f)
        self.es.close()
        return nc


T = 2304
TB = 256
NBLK = T // TB
D = 2048
DCH = 16
EPS = 1e-6
FAST = False
BF16 = mybir.dt.bfloat16


def norm_mod_T(kb, xT_d, hT_d, nw_sb, sc_sb, sh_sb, ones_sb, seg_of_blk, tag):
    nc = kb.nc
    nseg = sc_sb.shape[2]
    A = kb.sb(tag + "_A", [128, DCH, nseg])
    kb.ts(A[:, :, :], sc_sb[:, :, :], 1.0, ALU.add)
    kb.tt(A[:, :, :], A[:, :, :], nw_sb[:, :].unsqueeze(2).to_broadcast([128, DCH, nseg]), ALU.mult)
    xb = [kb.sb(tag + "_xb%d" % i, [128, DCH, TB]) for i in range(2)]
    hb = [kb.sb(tag + "_hb%d" % i, [128, DCH, TB], BF16) for i in range(2)]
    htmp = [kb.sb(tag + "_htmp%d" % i, [128, TB]) for i in range(2)]
    sq = kb.sb(tag + "_sq", [128, DCH, TB])
    rstd = kb.sb(tag + "_rstd", [128, TB])
    ps = kb.ps(tag + "_ps", [128, TB])
    xv = xT_d.ap().rearrange("(c p) t -> p c t", p=128)
    hv = hT_d.ap().rearrange("(c p) t -> p c t", p=128)
    for blk, seg in enumerate(seg_of_blk):
        x = xb[blk % 2]
        h = hb[blk % 2]
        kb.dma('sp', x[:, :, :], xv[:, :, blk * TB:(blk + 1) * TB])
        kb.act(sq[:, :, :], x[:, :, :], AF.Square)
        for c in range(DCH):
            kb.mm(ps[:, :], ones_sb[:, :], sq[:, c, :], start=(c == 0), stop=(c == DCH - 1))
        kb.ts(rstd[:, :], ps[:, :], 1.0 / D, ALU.mult, EPS, ALU.add)
        kb.act(rstd[:, :], rstd[:, :], AF.Sqrt)
        kb.recip(rstd[:, :], rstd[:, :])
        for c in range(DCH):
            ht = htmp[c % 2]
            kb.tt(ht[:, :], x[:, c, :], rstd[:, :], ALU.mult)
            kb.act(h[:, c, :], ht[:, :], AF.Identity, bias=sh_sb[:, c, seg:seg + 1], scale=A[:, c, seg:seg + 1])
        kb.dma('sp', hv[:, :, blk * TB:(blk + 1) * TB], h[:, :, :])


def in_proj(kb, hT_d, w_d, ncols_fm, zT_d, ncols_tm, ztok_d, ntok, tag):
    hv = hT_d.ap().rearrange("(c p) t -> p c t", p=128)
    wv = w_d.ap().rearrange("(c p) n -> p c n", p=128)
    wt = [kb.sb(tag + "_w%d" % i, [128, DCH, 512], BF16) for i in range(2)]
    hb = [kb.sb(tag + "_ih%d" % i, [128, DCH, TB], BF16) for i in range(3)]
    zs = [kb.sb(tag + "_zs%d" % i, [128, ntok]) for i in range(4)]
    pz = [kb.ps(tag + "_pz%d" % i, [128, 512]) for i in range(2)]
    nblk = ntok // TB
    assert ncols_fm % 128 == 0
    ngrp = (ncols_fm + 511) // 512
    it = 0
    pi = 0
    for g in range(ngrp):
        c0 = g * 512
        nc_ = min(512, ncols_fm - c0)
        w = wt[g % 2]
        kb.dma('pool', w[:, :, :nc_], wv[:, :, c0:c0 + nc_])
        for blk in range(nblk):
            h = hb[it % 3]
            it += 1
            kb.dma('act', h[:, :, :], hv[:, :, blk * TB:(blk + 1) * TB])
            for cc in range(nc_ // 128):
                p = pz[pi % 2]
                pi += 1
                for c in range(DCH):
                    kb.mm(p[:, :TB], w[:, c, cc * 128:(cc + 1) * 128], h[:, c, :], start=(c == 0), stop=(c == DCH - 1), fast=FAST)
                kb.copy(zs[cc][:, blk * TB:(blk + 1) * TB], p[:, :TB], e='act' if cc % 2 else 'dve')
        for cc in range(nc_ // 128):
            kb.dma('sp', zT_d.ap()[c0 + cc * 128:c0 + (cc + 1) * 128, :], zs[cc][:, :])
    ntg = (ncols_tm + 511) // 512
    zt = [kb.sb(tag + "_zt%d" % i, [128, 512]) for i in range(2)]
    zi = 0
    for g in range(ntg):
        c0 = g * 512
        nc_ = min(512, ncols_tm - c0)
        w = wt[(ngrp + g) % 2]
        kb.dma('pool', w[:, :, :nc_], wv[:, :, ncols_fm + c0:ncols_fm + c0 + nc_])
        for blk in range(nblk):
            h = hb[it % 3]
            it += 1
            kb.dma('act', h[:, :, :], hv[:, :, blk * TB:(blk + 1) * TB])
            for tt_ in range(TB // 128):
                p = pz[pi % 2]
                pi += 1
                for c in range(DCH):
                    kb.mm(p[:, :nc_], h[:, c, tt_ * 128:(tt_ + 1) * 128], w[:, c, :nc_], start=(c == 0), stop=(c == DCH - 1), fast=FAST)
                z = zt[zi % 2]
                zi += 1
                kb.copy(z[:, :nc_], p[:, :nc_], e='act' if zi % 2 else 'dve')
                t0 = blk * TB + tt_ * 128
                kb.dma('sp', ztok_d.ap()[t0:t0 + 128, c0:c0 + nc_], z[:, :nc_])


T = 2304
TC = 256
ROWS = 32
GW = 64


def natten(kb, zT_d, qrow0, ztok_d, vcol0, bias_d, mask_d, y_d, ycol0, tag):
    qT = [kb.sb(tag + "_q%d" % i, [128, T]) for i in range(2)]
    kT = [kb.sb(tag + "_k%d" % i, [128, T]) for i in range(2)]
    for i in range(2):
        kb.dma('sp', qT[i][:, :], zT_d.ap()[qrow0 + i * 128: qrow0 + (i + 1) * 128, :])
        kb.dma('act', kT[i][:, :], zT_d.ap()[qrow0 + 256 + i * 128: qrow0 + 256 + (i + 1) * 128, :])
        kb.ts(qT[i][:, :], qT[i][:, :], 0.125, ALU.mult)
    vl = kb.sb(tag + "_vl", [64, ROWS, 4, 65])
    vc = kb.sb(tag + "_vc", [128, 2, 4, 65])
    kb.memset(vl[:, :, :, 64:65], 1.0)
    kb.memset(vc[:, :, :, 64:65], 1.0)
    vsrc = ztok_d.ap()[:, vcol0:vcol0 + 256]
    for hh in range(4):
        kb.dma('sp', vl[:, :, hh, 0:64], vsrc[TC:T, hh * 64:(hh + 1) * 64].rearrange("(r k) d -> k r d", k=64))
        kb.dma('sp', vc[:, :, hh, 0:64], vsrc[0:TC, hh * 64:(hh + 1) * 64].rearrange("(r k) d -> k r d", k=128))
    bias = kb.sb(tag + "_bias", [64, 4, 15, 64])
    mask = kb.sb(tag + "_mask", [64, 64])
    kb.dma('sp', bias[:, :, :, :], bias_d.ap())
    kb.dma('sp', mask[:, :], mask_d.ap())
    kb.tt(bias[:, :, :, :], bias[:, :, :, :], mask[:, :].unsqueeze(1).unsqueeze(1).to_broadcast([64, 4, 15, 64]), ALU.add)
    pw = [kb.ps(tag + "_pw%d" % i, [64, 8, 64]) for i in range(2)]
    pc = [kb.ps(tag + "_pc%d" % i, [128, 2, 64]) for i in range(2)]
    py = [kb.ps(tag + "_py%d" % i, [64, 65]) for i in range(2)]
    ew = [kb.sb(tag + "_ew%d" % i, [64, 8, 64]) for i in range(2)]
    ec = [kb.sb(tag + "_ec%d" % i, [128, 2, 64]) for i in range(2)]
    ys = [kb.sb(tag + "_ys%d" % i, [64, 64]) for i in range(2)]
    rd = [kb.sb(tag + "_rd%d" % i, [64, 1]) for i in range(2)]
    it = 0
    for hh in range(4):
        ci, pb = hh // 2, (hh % 2) * 64
        q_h = qT[ci]
        k_h = kT[ci]
        for qi in range(4 + ROWS):
            i2 = it % 2
            it += 1
            q0 = qi * 64 if qi < 4 else TC + (qi - 4) * 64
            qs = q_h[pb:pb + 64, q0:q0 + 64]
            for kt in range(2):
                kb.mm(pc[i2][:, kt, :], k_h[pb:pb + 64, kt * 128:(kt + 1) * 128], qs)
            kb.act(ec[i2][:, :, :], pc[i2][:, :, :], AF.Exp)
            nk = 2
            if qi >= 4:
                r = qi - 4
                kr0 = min(max(r - 4, 0), ROWS - 8)
                for j in range(8):
                    kk0 = TC + (kr0 + j) * 64
                    kb.mm(pw[i2][:, j, :], k_h[pb:pb + 64, kk0:kk0 + 64], qs)
                dr0 = kr0 - r + 7
                kb.tt(ew[i2][:, :, :], pw[i2][:, :, :], bias[:, hh, dr0:dr0 + 8, :], ALU.add)
                kb.act(ew[i2][:, :, :], ew[i2][:, :, :], AF.Exp)
                nk = 10
            n = 0
            for kt in range(2):
                kb.mm(py[i2][:, :], ec[i2][:, kt, :], vc[:, kt, hh, :], start=(n == 0), stop=(n == nk - 1))
                n += 1
            if qi >= 4:
                for j in range(8):
                    kb.mm(py[i2][:, :], ew[i2][:, j, :], vl[:, kr0 + j, hh, :], start=False, stop=(n == nk - 1))
                    n += 1
            kb.recip(rd[i2][:, :], py[i2][:, 64:65])
            kb.ts(ys[i2][:, :], py[i2][:, 0:64], rd[i2][:, 0:1], ALU.mult)
            kb.dma('sp', y_d.ap()[q0:q0 + 64, ycol0 + hh * 64: ycol0 + (hh + 1) * 64], ys[i2][:, :])


def na_host_tables(rpb_l, hsel):
    kc = np.arange(64)[:, None]
    qc = np.arange(64)[None, :]
    idx = np.clip(kc - qc + 15, 0, 30)
    g = rpb_l[hsel][:, :, idx]
    g = np.ascontiguousarray(np.transpose(g, (2, 0, 1, 3))).astype(np.float32)
    c0 = np.clip(qc - 8, 0, 48)
    inw = (kc >= c0) & (kc < c0 + 16)
    mask = np.where(inw, 0.0, -1e30).astype(np.float32)
    return g, mask


T = 2304
TC = 256
SEGS = [(0, 256), (256, 2304)]
TBLK = [(i * 512, min(512, T - i * 512)) for i in range(5)]
WDS = 0.606531
TCH = 16
ORDER = 4
R32 = True
F32R = mybir.dt.float32r
KEEP = 3
NARR = 26


def rwkv(kb, zT_d, row0, prm, rws_d, yT_d, yrow0, tag, consts):
    bones, ident2 = consts['bones'], consts['ident2']
    R = rws_d.ap()
    kb.push()
    P = {}
    for k_, shp in (('mu', [128, 9, 2]), ('w0', [128, 2, 2]), ('a0', [128, 2, 2]), ('kk', [128, 2]), ('ka', [128, 2]),
                    ('w2', [128, 256]), ('a2', [128, 256]), ('g2', [128, 256])):
        P[k_] = kb.sb(tag + "_p_" + k_, shp)
        kb.dma('sp', P[k_][tuple(slice(None) for _ in shp)], prm[k_].ap())
    c0 = kb.sb(tag + "_c0", [128, 9])
    kb.tt(c0[:, :], P['mu'][:, :, 0], P['mu'][:, :, 1], ALU.add)
    kb.ts(c0[:, :], c0[:, :], -1.0, ALU.mult, 1.0, ALU.add)
    omka = kb.sb(tag + "_omka", [128, 2])
    kb.ts(omka[:, :], P['ka'][:, :], -1.0, ALU.mult, 1.0, ALU.add)
    zs = kb.sb(tag + "_zs", [128, 9, T])
    zin = [kb.sb(tag + "_zin%d" % i, [128, T]) for i in range(2)]
    for c in range(9):
        z = zin[c % 2]
        kb.dma('sp', z[:, :], zT_d.ap()[row0 + c * 128: row0 + (c + 1) * 128, :])
        kb.ts(zs[:, c, :], z[:, :], c0[:, c:c + 1], ALU.mult, e='pool' if c % 2 else 'dve')
        for (s0, s1) in SEGS:
            kb.stt(zs[:, c, s0 + 1:s1], z[:, s0:s1 - 1], P['mu'][:, c, 0:1], zs[:, c, s0 + 1:s1], ALU.mult, ALU.add)
            kb.stt(zs[:, c, s0:s1 - 1], z[:, s0 + 1:s1], P['mu'][:, c, 1:2], zs[:, c, s0:s1 - 1], ALU.mult, ALU.add)
    kb.act(zs[:, 6, :], zs[:, 6, :], AF.Tanh)
    kb.act(zs[:, 8, :], zs[:, 8, :], AF.Sigmoid)
    pp = [kb.ps(tag + "_pp%d" % i, [128, 512]) for i in range(4)]
    pi = [0]

    def nps():
        pi[0] += 1
        return pp[pi[0] % 4]
    ta = kb.sb(tag + "_ta", [128, T])
    tb_ = kb.sb(tag + "_tb", [128, T])
    a_sb = [kb.sb(tag + "_a%d" % d, [128, T]) for d in range(2)]
    kk_sb = kb.sb(tag + "_kk", [128, T])
    for c2 in range(2):
        cs = slice(c2 * 128, (c2 + 1) * 128)
        kb.dma('sp', R[c2 * 3 + 1], zs[:, 0 + c2, :])
        kb.dma('sp', R[c2 * 3 + 2], zs[:, 4 + c2, :])
        kb.dma('sp', R[20 + c2], zs[:, 2 + c2, :])
        for d in range(2):
            ds = slice(64 * d, 64 * d + 64)
            for (t0, n) in TBLK:
                p = nps()
                kb.mm(p[:, :n], P['w2'][ds, cs], zs[ds, 6, t0:t0 + n])
                kb.act(ta[:, t0:t0 + n], p[:, :n], AF.Sigmoid, bias=P['w0'][:, c2, d:d + 1])
                p = nps()
                kb.mm(p[:, :n], P['a2'][ds, cs], zs[ds, 7, t0:t0 + n])
                kb.act(a_sb[d][:, t0:t0 + n], p[:, :n], AF.Sigmoid, bias=P['a0'][:, c2, d:d + 1])
            kb.act(ta[:, :], ta[:, :], AF.Exp, scale=-WDS)
            kb.dma('sp', R[6 + (c2 * 2 + d) * 3 + 0], ta[:, :])
        for (t0, n) in TBLK:
            p = nps()
            kb.mm(p[:, :n], P['g2'][:, cs], zs[:, 8, t0:t0 + n])
            kb.copy(ta[:, t0:t0 + n], p[:, :n], e='act')
        kb.dma('sp', R[18 + c2], ta[:, :])
        kb.ts(kk_sb[:, :], zs[:, 2 + c2, :], P['kk'][:, c2:c2 + 1], ALU.mult)
        kb.tt(tb_[:, :], kk_sb[:, :], kk_sb[:, :], ALU.mult, e='pool')
        for (t0, n) in TBLK:
            p = nps()
            kb.mm(p[:, :n], bones[:, :], tb_[:, t0:t0 + n])
            kb.act(ta[:, t0:t0 + n], p[:, :n], AF.Sqrt)
        kb.ts(ta[:, :], ta[:, :], 1e-12, ALU.max)
        kb.recip(ta[:, :], ta[:, :])
        kb.tt(kk_sb[:, :], kk_sb[:, :], ta[:, :], ALU.mult)
        kb.ts(tb_[:, :], kk_sb[:, :], -1.0, ALU.mult)
        kb.dma('sp', R[c2 * 3 + 0], tb_[:, :])
        for d in range(2):
            kb.tt(ta[:, :], kk_sb[:, :], a_sb[d][:, :], ALU.mult)
            kb.dma('sp', R[6 + (c2 * 2 + d) * 3 + 1], ta[:, :])
            kb.ts(tb_[:, :], a_sb[d][:, :], P['ka'][:, c2:c2 + 1], ALU.mult, omka[:, c2:c2 + 1], ALU.add)
            kb.tt(tb_[:, :], tb_[:, :], zs[:, 2 + c2, :], ALU.mult)
            kb.dma('sp', R[6 + (c2 * 2 + d) * 3 + 2], tb_[:, :])
    kb.pop()
    kb.push()
    rwkv_scan(kb, rws_d, consts, tag)
    kb.pop()
    kb.push()
    Q = {}
    for k_ in ('rk', 'lnw', 'lnb'):
        Q[k_] = kb.sb(tag + "_q_" + k_, [128, 2])
        kb.dma('sp', Q[k_][:, :], prm[k_].ap())
    pp = [kb.ps(tag + "_op%d" % i, [128, 512]) for i in range(4)]
    y = kb.sb(tag + "_y", [128, T])
    y2 = kb.sb(tag + "_y2", [128, T])
    t1 = kb.sb(tag + "_t1", [128, T])
    t2 = kb.sb(tag + "_t2", [128, T])
    t3 = kb.sb(tag + "_t3", [128, T])
    for c2 in range(2):
        kb.dma('sp', y[:, :], R[22 + 0 * 2 + c2])
        kb.dma('act', y2[:, :], R[22 + 1 * 2 + c2])
        kb.tt(y[:, :], y[:, :], y2[:, :], ALU.add)
        for (t0, n) in TBLK:
            p = nps()
            kb.mm(p[:, :n], bones[:, :], y[:, t0:t0 + n])
            kb.stt(t1[:, t0:t0 + n], p[:, :n], -1.0 / 64, y[:, t0:t0 + n], ALU.mult, ALU.add)
        kb.tt(t2[:, :], t1[:, :], t1[:, :], ALU.mult, e='pool')
        for (t0, n) in TBLK:
            p = nps()
            kb.mm(p[:, :n], bones[:, :], t2[:, t0:t0 + n])
            kb.ts(t3[:, t0:t0 + n], p[:, :n], 1.0 / 64, ALU.mult, 64e-5, ALU.add)
        kb.act(t3[:, :], t3[:, :], AF.Sqrt)
        kb.recip(t3[:, :], t3[:, :])
        kb.tt(t1[:, :], t1[:, :], t3[:, :], ALU.mult)
        kb.ts(t1[:, :], t1[:, :], Q['lnw'][:, c2:c2 + 1], ALU.mult, Q['lnb'][:, c2:c2 + 1], ALU.add)
        kb.dma('sp', y[:, :], R[c2 * 3 + 1])
        kb.dma('act', y2[:, :], R[20 + c2])
        kb.stt(t2[:, :], y[:, :], Q['rk'][:, c2:c2 + 1], y2[:, :], ALU.mult, ALU.mult)
        kb.dma('sp', y2[:, :], R[c2 * 3 + 2])
        for (t0, n) in TBLK:
            p = nps()
            kb.mm(p[:, :n], bones[:, :], t2[:, t0:t0 + n])
            kb.tt(t3[:, t0:t0 + n], p[:, :n], y2[:, t0:t0 + n], ALU.mult)
        kb.tt(t1[:, :], t1[:, :], t3[:, :], ALU.add)
        kb.dma('sp', y[:, :], R[18 + c2])
        kb.tt(t1[:, :], t1[:, :], y[:, :], ALU.mult)
        kb.dma('sp', yT_d.ap()[yrow0 + c2 * 128: yrow0 + (c2 + 1) * 128, :], t1[:, :])
    kb.pop()


def rwkv_scan(kb, rws_d, consts, tag):
    bones, ident2 = consts['bones'], consts['ident2']
    R = rws_d.ap()
    nchunk = T // TCH
    nctx = TC // TCH
    G = [(d, c2) for d in range(2) for c2 in range(2)]
    S = [[kb.sb(tag + "_S%d_%d" % (g, i), [128, 64]) for i in range(2)] for g in range(4)]
    zt_ = kb.sb(tag + "_zt", [128, 64])
    kb.memset(zt_[:, :], 0.0)
    for g in range(4):
        if R32:
            kb.copy(S[g][0][:, :].bitcast(F32R), zt_[:, :])
        else:
            kb.memset(S[g][0][:, :], 0.0)
    sh = [[kb.sb(tag + "_sh%d_%d" % (g, i), [128, 3, TCH]) for i in range(2)] for g in range(4)]
    pd = [[kb.sb(tag + "_pd%d_%d" % (g, i), [128, 3, TCH]) for i in range(2)] for g in range(4)]
    Vd = [kb.sb(tag + "_Vd%d" % i, [128, TCH, 64]) for i in range(2)]
    KV = [[kb.sb(tag + "_KV%d_%d" % (g, i), [128, TCH, 64]) for i in range(2)] for g in range(4)]
    Abc = [[kb.sb(tag + "_Ab%d_%d" % (g, i), [128, TCH, 128 if R32 else 64]) for i in range(2)] for g in range(4)]
    tmp = [kb.sb(tag + "_tmp%d" % g, [128, 64]) for g in range(4)]
    ysb = [kb.sb(tag + "_ysb%d" % i, [128, 4, TCH]) for i in range(2)]
    pvb = [kb.ps(tag + "_pvb%d" % i, [128, TCH * 64]) for i in range(1)]
    psa = [kb.ps(tag + "_psa%d" % i, [128, 512]) for i in range(4)]
    pys = [kb.ps(tag + "_pys%d" % i, [128, 4, 128]) for i in range(2)]
    par = [0, 0, 0, 0]
    vi = 0
    pend = []
    late = []
    for n in range(nchunk):
        nb = n % 2
        t0s = []
        for g, (d, c2) in enumerate(G):
            if d == 0:
                t0 = n * TCH
            else:
                t0 = (nctx - 1 - n) * TCH if n < nctx else (nchunk - 1 - (n - nctx)) * TCH
            t0s.append(t0)
            kb.dma('sp', sh[g][nb][:, :, :], R[c2 * 3: c2 * 3 + 3, :, t0:t0 + TCH].rearrange("a p t -> p a t"))
            b0 = 6 + (c2 * 2 + d) * 3
            kb.dma('act', pd[g][nb][:, :, :], R[b0: b0 + 3, :, t0:t0 + TCH].rearrange("a p t -> p a t"))
            vd = Vd[vi % 2]
            pv = pvb[0]
            vi += 1
            kb.tt(vd[:, :, :], ident2[:, :].unsqueeze(1).to_broadcast([128, TCH, 64]),
                  sh[g][nb][:, 2, :].unsqueeze(2).to_broadcast([128, TCH, 64]), ALU.mult, e='pool')
            vflat = vd[:, :, :].rearrange("p t i -> p (t i)")
            for j in range(TCH * 64 // 512):
                kb.mm(pv[:, j * 512:(j + 1) * 512], bones[:, :], vflat[:, j * 512:(j + 1) * 512])
            kb.tt(KV[g][nb][:, :, :], pv[:, :].rearrange("p (t i) -> p t i", i=64),
                  pd[g][nb][:, 2, :].unsqueeze(2).to_broadcast([128, TCH, 64]), ALU.mult)
            if R32:
                kb.tt(Abc[g][nb][:, :, :].bitcast(F32R), sh[g][nb][:, 0, :].unsqueeze(2).to_broadcast([128, TCH, 128]),
                      bones[:, :].unsqueeze(1).to_broadcast([128, TCH, 128]), ALU.mult, e='pool')
            else:
                kb.copy(Abc[g][nb][:, :, :], sh[g][nb][:, 0, :].unsqueeze(2).to_broadcast([128, TCH, 64]), e='pool')
        yp = pys[nb]

        def emit_sa(g, s):
            d, c2 = G[g]
            si = s if d == 0 else TCH - 1 - s
            Sc = S[g][par[g]]
            Sn = S[g][1 - par[g]]
            par[g] = 1 - par[g]
            if R32:
                kb.mm(psa[g][:, 0:64], Abc[g][nb][:, si, :].bitcast(F32R), Sc[:, :].bitcast(F32R), r=[Abc[g][nb], Sc], w=[psa[g]])
            else:
                for h in range(2):
                    hs = slice(64 * h, 64 * h + 64)
                    kb.mm(psa[g][hs, 0:64], Abc[g][nb][hs, si, :], Sc[hs, :], r=[Abc[g][nb], Sc], w=[psa[g]])
            return (g, Sc, Sn, si)

        def flush_one():
            if not pend:
                return
            (g_, Sn_, si_, yp_, sh_, nb_) = pend.pop(0)
            for h in range(2):
                hs = slice(64 * h, 64 * h + 64)
                kb.mm(yp_[hs, g_, si_:si_ + 1], Sn_[hs, :], sh_[hs, 1, si_:si_ + 1], r=[Sn_, sh_], w=[('y', nb_, g_)])
        for s in range(TCH):
            items = []
            for g in range(4):
                it = emit_sa(g, s)
                flush_one()
                (g_, Sc, Sn, si) = it
                kb.stt(tmp[g][:, :], Sc[:, :], pd[g][nb][:, 0, si:si + 1], KV[g][nb][:, si, :], ALU.mult, ALU.add)
                items.append(it)
            if s == 0 and late:
                (pnb, pt0s) = late.pop()
                kb.copy(ysb[pnb][:, :, :], pys[pnb][:, :, 0:TCH], e='act', r=[('y', pnb, g) for g in range(4)], w=[ysb[pnb]])
                for g, (d, c2) in enumerate(G):
                    kb.dma('sp', R[22 + d * 2 + c2, :, pt0s[g]:pt0s[g] + TCH], ysb[pnb][:, g, :])
            for (g, Sc, Sn, si) in items:
                kb.stt(Sn[:, :].bitcast(F32R) if R32 else Sn[:, :], psa[g][:, 0:64], pd[g][nb][:, 1, si:si + 1], tmp[g][:, :], ALU.mult, ALU.add)
                pend.append((g, Sn, si, yp, sh[g][nb], nb))
        late.append((nb, list(t0s)))
    while pend:
        flush_one()
    (pnb, pt0s) = late.pop()
    kb.copy(ysb[pnb][:, :, :], pys[pnb][:, :, 0:TCH], e='act', r=[('y', pnb, g) for g in range(4)], w=[ysb[pnb]])
    for g, (d, c2) in enumerate(G):
        kb.dma('sp', R[22 + d * 2 + c2, :, pt0s[g]:pt0s[g] + TCH], ysb[pnb][:, g, :])
    return
    if False:
        nb = 0
        t0s = [0] * 4
        kb.copy(ysb[nb][:, :, :], yp[:, :, 0:TCH], e='act', r=[('y', nb, g) for g in range(4)], w=[ysb[nb]])
        for g, (d, c2) in enumerate(G):
            kb.dma('sp', R[22 + d * 2 + c2, :, t0s[g]:t0s[g] + TCH], ysb[nb][:, g, :])


T = 2304
TC = 256
NT = 18
SEGS = [(0, 256), (256, 2304)]
TBLK = [(i * 512, min(512, T - i * 512)) for i in range(5)]
EPS = 1e-6


def mamba(kb, zT_d, row0, ztok_d, col0, prm, cst, y_d, ycol0, tag):
    ident = cst['ident']
    xs_tok = kb.sb(tag + "_xstok", [128, NT, 512])
    B_tok = kb.sb(tag + "_Btok", [128, NT, 256])
    BT = [kb.sb(tag + "_BT%d" % g, [128, T]) for g in range(2)]
    CT = [kb.sb(tag + "_CT%d" % g, [128, T]) for g in range(2)]
    kb.push()
    cw = kb.sb(tag + "_cw", [128, 8, 5])
    cb = kb.sb(tag + "_cb", [128, 8])
    kb.dma('sp', cw[:, :, :], prm['cw'].ap())
    kb.dma('sp', cb[:, :], prm['cb'].ap())
    pmT = kb.sb(tag + "_pmT", [128, 128])
    cos = kb.sb(tag + "_cos", [128, 2048])
    sin = kb.sb(tag + "_sin", [128, 2048])
    kb.dma('sp', pmT[:, :], cst['pmT'].ap())
    kb.dma('sp', cos[:, :], cst['cos'].ap())
    kb.dma('act', sin[:, :], cst['sin'].ap())
    zin = [kb.sb(tag + "_zin%d" % i, [128, T]) for i in range(2)]
    cv = [kb.sb(tag + "_cv%d" % i, [128, T]) for i in range(2)]
    tr = kb.sb(tag + "_tr", [128, 2048])
    pp = [kb.ps(tag + "_pp%d" % i, [128, 512]) for i in range(4)]
    pi = [0]

    def nps():
        pi[0] += 1
        return pp[pi[0] % 4]
    for c in range(8):
        z = zin[c % 2]
        kb.dma('sp', z[:, :], zT_d.ap()[row0 + c * 128: row0 + (c + 1) * 128, :])
        if c < 4:
            o = cv[c % 2]
        elif c < 6:
            o = BT[c - 4]
        else:
            o = CT[c - 6]
        kb.ts(o[:, :], z[:, :], cw[:, c, 2:3], ALU.mult, cb[:, c:c + 1], ALU.add)
        for (s0, s1) in SEGS:
            for i in (0, 1, 3, 4):
                of = i - 2
                a0, a1 = s0 + max(0, -of), s1 - max(0, of)
                kb.stt(o[:, a0:a1], z[:, a0 + of:a1 + of], cw[:, c, i:i + 1], o[:, a0:a1], ALU.mult, ALU.add)
        kb.act(o[:, :], o[:, :], AF.Silu)
        if c >= 4:
            for j in range(4):
                p = nps()
                kb.mm(p[:, :], pmT[:, :], o[:, TC + j * 512: TC + (j + 1) * 512])
                kb.tt(tr[:, j * 512:(j + 1) * 512], p[:, :], sin[:, j * 512:(j + 1) * 512], ALU.mult)
            kb.tt(o[:, TC:T], o[:, TC:T], cos[:, :], ALU.mult, e='pool')
            kb.tt(o[:, TC:T], o[:, TC:T], tr[:, :], ALU.add)
        if c < 6:
            for ti in range(NT):
                p = nps()
                kb.transpose(p[:, 0:128], o[:, ti * 128:(ti + 1) * 128], ident[:, :])
                dst = xs_tok[:, ti, c * 128:(c + 1) * 128] if c < 4 else B_tok[:, ti, (c - 4) * 128:(c - 3) * 128]
                kb.copy(dst, p[:, 0:128], e='act' if ti % 2 else 'dve')
    kb.pop()
    kb.push()
    y_acc = kb.sb(tag + "_yacc", [128, NT, 512])
    dtr = kb.sb(tag + "_dtr", [128, NT, 16])
    kb.dma('sp', dtr[:, :, :], ztok_d.ap()[:, col0 + 512: col0 + 528].rearrange("(n p) c -> p n c", p=128))
    dtb = kb.sb(tag + "_dtb", [128, 16])
    aneg = kb.sb(tag + "_aneg", [128, 16])
    kb.dma('sp', dtb[:, :], prm['dtb'].ap().partition_broadcast(128))
    kb.dma('sp', aneg[:, :], prm['alog'].ap().partition_broadcast(128))
    kb.act(aneg[:, :], aneg[:, :], AF.Exp)
    kb.ts(aneg[:, :], aneg[:, :], -1.0, ALU.mult)
    dt = kb.sb(tag + "_dt", [128, NT, 16])
    dA = kb.sb(tag + "_dA", [128, NT, 16])
    kb.tt(dt[:, :, :], dtr[:, :, :], dtb[:, :].unsqueeze(1).to_broadcast([128, NT, 16]), ALU.add)
    kb.act(dt[:, :, :], dt[:, :, :], AF.Exp)
    kb.ts(dt[:, :, :], dt[:, :, :], 1.0, ALU.add)
    kb.act(dt[:, :, :], dt[:, :, :], AF.Ln)
    kb.tt(dA[:, :, :], dt[:, :, :], aneg[:, :].unsqueeze(1).to_broadcast([128, NT, 16]), ALU.mult)
    tri = kb.sb(tag + "_tri", [128, 2, 128])
    nmask = kb.sb(tag + "_nmask", [128, 2, 128])
    kb.dma('sp', tri[:, :, :], cst['tri'].ap().rearrange("d p l -> p d l"))
    kb.dma('sp', nmask[:, :, :], cst['nmask'].ap().rearrange("d p l -> p d l"))
    Hs = [kb.sb(tag + "_H%d" % d, [128, 8, 64]) for d in range(2)]
    for d in range(2):
        kb.memset(Hs[d][:, :, :], 0.0)
    p_acs = kb.ps(tag + "_pacs", [128, 8, 128])
    p_col = kb.ps(tag + "_pcol", [128, 8])
    p_G = kb.ps(tag + "_pG", [128, 2, 128])
    p_y = kb.ps(tag + "_py", [128, 8, 64])
    p_yo = kb.ps(tag + "_pyo", [128, 8, 64])
    p_H = kb.ps(tag + "_pH", [128, 8, 64])
    dAb = kb.sb(tag + "_dAb", [128, 8, 128])
    acol = kb.sb(tag + "_acol", [128, 8])
    DT_ = kb.sb(tag + "_DT", [128, 8, 128])
    WT = kb.sb(tag + "_WT", [128, 8, 128])
    xdt = kb.sb(tag + "_xdt", [128, 8, 64])
    xdts = kb.sb(tag + "_xdts", [128, 8, 64])
    eac = kb.sb(tag + "_eac", [128, 8])
    decs = kb.sb(tag + "_decs", [128, 8])
    etot = kb.sb(tag + "_etot", [128, 8])
    ytmp = kb.sb(tag + "_ytmp", [128, 8, 64])
    order = [list(range(NT)), [1, 0] + list(range(NT - 1, 1, -1))]
    seen = set()
    for n in range(NT):
        for d in range(2):
            ti = order[d][n]
            tsl = slice(ti * 128, (ti + 1) * 128)
            hsl = slice(d * 8, d * 8 + 8)
            llast = 127 if d == 0 else 0
            kb.mm(p_col[:, :], tri[:, d, :], dA[:, ti, hsl])
            kb.copy(acol[:, :], p_col[:, :], e='act')
            kb.copy(dAb[:, :, :], dA[:, ti, hsl].unsqueeze(2).to_broadcast([128, 8, 128]), e='pool')
            for h in range(8):
                kb.mm(p_acs[:, h, :], dAb[:, h, :], tri[:, d, :])
            kb.tt(DT_[:, :, :], p_acs[:, :, :], nmask[:, d, :].unsqueeze(1).to_broadcast([128, 8, 128]), ALU.add)
            kb.tt(DT_[:, :, :], DT_[:, :, :], acol[:, :].unsqueeze(2).to_broadcast([128, 8, 128]), ALU.subtract)
            kb.act(DT_[:, :, :], DT_[:, :, :], AF.Exp)
            for g in range(2):
                kb.mm(p_G[:, g, :], BT[g][:, tsl], CT[g][:, tsl])
            for g in range(2):
                kb.tt(WT[:, g * 4:(g + 1) * 4, :], DT_[:, g * 4:(g + 1) * 4, :],
                      p_G[:, g, :].unsqueeze(1).to_broadcast([128, 4, 128]), ALU.mult)
            kb.tt(xdt[:, :, :], xs_tok[:, ti, :].rearrange("p (h q) -> p h q", q=64),
                  dt[:, ti, hsl].unsqueeze(2).to_broadcast([128, 8, 64]), ALU.mult, e='pool')
            for h in range(8):
                kb.mm(p_y[:, h, :], WT[:, h, :], xdt[:, h, :])
            for g in range(2):
                kb.mm(p_yo[:, g * 4:(g + 1) * 4, :], CT[g][:, tsl], Hs[d][:, g * 4:(g + 1) * 4, :])
            kb.act(eac[:, :], acol[:, :], AF.Exp)
            kb.tt(ytmp[:, :, :], p_yo[:, :, :], eac[:, :].unsqueeze(2).to_broadcast([128, 8, 64]), ALU.mult)
            ya = y_acc[:, ti, :].rearrange("p (h q) -> p h q", q=64)
            if ti in seen:
                kb.tt(ytmp[:, :, :], ytmp[:, :, :], p_y[:, :, :], ALU.add)
                kb.tt(ya, ya, ytmp[:, :, :], ALU.add, e='pool')
            else:
                kb.tt(ya, ytmp[:, :, :], p_y[:, :, :], ALU.add)
                seen.add(ti)
            kb.tt(decs[:, :], p_acs[:, :, llast], acol[:, :], ALU.subtract)
            kb.act(decs[:, :], decs[:, :], AF.Exp)
            kb.act(etot[:, :], p_acs[:, :, llast], AF.Exp)
            kb.tt(xdts[:, :, :], xdt[:, :, :], decs[:, :].unsqueeze(2).to_broadcast([128, 8, 64]), ALU.mult)
            for g in range(2):
                kb.mm(p_H[:, g * 4:(g + 1) * 4, :], B_tok[:, ti, g * 128:(g + 1) * 128], xdts[:, g * 4:(g + 1) * 4, :])
            kb.tt(Hs[d][:, :, :], Hs[d][:, :, :], etot[:, :].unsqueeze(2).to_broadcast([128, 8, 64]), ALU.mult)
            kb.tt(Hs[d][:, :, :], Hs[d][:, :, :], p_H[:, :, :], ALU.add)
    dsk = kb.sb(tag + "_dsk", [128, 8])
    nw = kb.sb(tag + "_nw", [128, 512])
    kb.dma('sp', dsk[:, :], prm['dsk'].ap().partition_broadcast(128))
    kb.dma('sp', nw[:, :], prm['nw'].ap().partition_broadcast(128))
    gt = [kb.sb(tag + "_gt%d" % i, [128, 512]) for i in range(2)]
    yo = [kb.sb(tag + "_yo%d" % i, [128, 512]) for i in range(2)]
    sq = kb.sb(tag + "_sq", [128, 512])
    ss = kb.sb(tag + "_ss", [128, 2])
    for ti in range(NT):
        g_ = gt[ti % 2]
        y_ = yo[ti % 2]
        kb.dma('act', g_[:, :], ztok_d.ap()[ti * 128:(ti + 1) * 128, col0:col0 + 512])
        kb.act(g_[:, :], g_[:, :], AF.Silu)
        kb.tt(y_[:, :].rearrange("p (h q) -> p h q", q=64), xs_tok[:, ti, :].rearrange("p (h q) -> p h q", q=64),
              dsk[:, :].unsqueeze(2).to_broadcast([128, 8, 64]), ALU.mult, e='pool')
        kb.tt(y_[:, :], y_[:, :], y_acc[:, ti, :], ALU.add)
        kb.tt(y_[:, :], y_[:, :], g_[:, :], ALU.mult)
        kb.tt(sq[:, :], y_[:, :], y_[:, :], ALU.mult, e='pool')
        kb.op('dve', lambda eng, o=ss, i=sq: eng.reduce_sum(o[:, :], i[:, :].rearrange("p (g q) -> p g q", q=256), axis=mybir.AxisListType.X),
              r=[sq], w=[ss])
        kb.ts(ss[:, :], ss[:, :], 1.0 / 256, ALU.mult, EPS, ALU.add)
        kb.act(ss[:, :], ss[:, :], AF.Sqrt)
        kb.recip(ss[:, :], ss[:, :])
        kb.tt(y_[:, :].rearrange("p (g q) -> p g q", q=256), y_[:, :].rearrange("p (g q) -> p g q", q=256),
              ss[:, :].unsqueeze(2).to_broadcast([128, 2, 256]), ALU.mult)
        kb.tt(y_[:, :], y_[:, :], nw[:, :], ALU.mult)
        kb.dma('sp', y_d.ap()[ti * 128:(ti + 1) * 128, ycol0:ycol0 + 512], y_[:, :])
    kb.pop()


def mb_consts():
    nf = 32
    inv = (10000.0 ** (-np.arange(nf, dtype=np.float32) / nf)).astype(np.float32)
    pos = np.arange(2048)
    row = (pos // 64).astype(np.float32)
    col = (pos % 64).astype(np.float32)
    cos = np.zeros((128, 2048), np.float32)
    sin = np.zeros((128, 2048), np.float32)
    for n in range(128):
        p = row if n < 64 else col
        ang = (p * inv[n % 32]).astype(np.float32)
        cos[n] = np.cos(ang)
        sin[n] = np.sin(ang)
    pm = np.zeros((128, 128), np.float32)
    for n in range(128):
        if (n % 64) < 32:
            pm[n, n + 32] = -1.0
        else:
            pm[n, n - 32] = 1.0
    s = np.arange(128)[:, None]
    l = np.arange(128)[None, :]
    tri = np.stack([(s <= l), (s >= l)]).astype(np.float32)
    nmask = np.where(tri > 0, 0.0, -1e30).astype(np.float32)
    return dict(pmT=np.ascontiguousarray(pm.T), cos=cos, sin=sin, tri=tri, nmask=nmask)


NTOK = 1152
NTI = 9
D = 2048
DCH = 16
EPS = 1e-6
P2BLK = [(0, 128, 0), (128, 512, 1), (640, 512, 1)]
KG = 2
NGRP = 128 // KG
TP = 384
NPASS = NTOK // TP


def p2(kb, yT_d, xT_d, prm, cst, xo_d, x1T_d, h2T_d, tag, final_nw=None, ntok=NTOK, halves=None, ctx_tiles=1):
    ones, ident = cst['ones'], cst['ident']
    if halves is None:
        halves = [(0, NTOK, P2BLK)]
    npass = ntok // TP
    xv = xT_d.ap().rearrange("(c p) t -> p c t", p=128)
    yv = yT_d.ap().rearrange("(c p) t -> p c t", p=128)
    x1v = x1T_d.ap().rearrange("(c p) t -> p c t", p=128)
    h2v = h2T_d.ap().rearrange("(c p) t -> p c t", p=128)
    xov = xo_d.ap().rearrange("(c p) t -> p c t", p=128)
    M = {}
    for k_ in ('g1', 'sh2', 'sc2', 'g2'):
        M[k_] = kb.sb(tag + "_m_" + k_, [128, DCH, 2])
        kb.dma('sp', M[k_][:, :, :], prm[k_].ap())
    nw2 = kb.sb(tag + "_nw2", [128, DCH])
    kb.dma('sp', nw2[:, :], prm['nw2'].ap())
    kb.push()
    xT = kb.sb(tag + "_xT", [128, DCH, NTOK])
    yT = kb.sb(tag + "_yT", [128, DCH, NTOK])
    wt = [kb.sb(tag + "_wo%d" % i, [128, DCH, 128]) for i in range(2)]
    pz = [kb.ps(tag + "_pz%d" % i, [128, 512]) for i in range(4)]
    wv = prm['wout'].ap().rearrange("(c p) n -> p c n", p=128)
    A = kb.sb(tag + "_A", [128, DCH, 2])
    kb.ts(A[:, :, :], M['sc2'][:, :, :], 1.0, ALU.add)
    kb.tt(A[:, :, :], A[:, :, :], nw2[:, :].unsqueeze(2).to_broadcast([128, DCH, 2]), ALU.mult)
    sq = kb.sb(tag + "_sq", [128, 512])
    rstd = kb.sb(tag + "_rstd", [128, 512])
    pi = 0
    wi = 0
    for (off, nh, blks) in halves:
        for c in range(DCH):
            kb.dma('sp' if c % 2 else 'act', xT[:, c, :nh], xv[:, c, off:off + nh])
            kb.dma('act' if c % 2 else 'sp', yT[:, c, :nh], yv[:, c, off:off + nh])
        for dc in range(DCH):
            w = wt[wi % 2]
            wi += 1
            kb.dma('sp', w[:, :, :], wv[:, :, dc * 128:(dc + 1) * 128])
            for (t0, n, seg) in blks:
                p = pz[pi % 4]
                pi += 1
                for mc in range(DCH):
                    kb.mm(p[:, :n], w[:, mc, :], yT[:, mc, t0:t0 + n], start=(mc == 0), stop=(mc == DCH - 1))
                kb.stt(xT[:, dc, t0:t0 + n], p[:, :n], M['g1'][:, dc, seg:seg + 1], xT[:, dc, t0:t0 + n], ALU.mult, ALU.add)
        kb.dma('sp', x1v[:, :, off:off + nh], xT[:, :, :nh])
        for (t0, n, seg) in blks:
            p = pz[pi % 4]
            pi += 1
            for c in range(DCH):
                s_ = sq if c % 2 == 0 else rstd
                kb.act(s_[:, :n], xT[:, c, t0:t0 + n], AF.Square)
                kb.mm(p[:, :n], ones[:, :], s_[:, :n], start=(c == 0), stop=(c == DCH - 1))
            kb.ts(rstd[:, :n], p[:, :n], 1.0 / D, ALU.mult, EPS, ALU.add)
            kb.act(rstd[:, :n], rstd[:, :n], AF.Sqrt)
            kb.recip(rstd[:, :n], rstd[:, :n])
            for c in range(DCH):
                kb.tt(yT[:, c, t0:t0 + n], xT[:, c, t0:t0 + n], rstd[:, :n], ALU.mult)
                kb.act(yT[:, c, t0:t0 + n], yT[:, c, t0:t0 + n], AF.Identity, bias=M['sh2'][:, c, seg:seg + 1], scale=A[:, c, seg:seg + 1])
        kb.dma('sp', h2v[:, :, off:off + nh], yT[:, :, :nh])
    kb.pop()
    kb.push()
    keysT = kb.sb(tag + "_keysT", [128, 16, 128])
    kb.dma('sp', keysT[:, :, :], prm['keysT'].ap())
    h2 = kb.sb(tag + "_h2", [128, DCH, TP])
    S = kb.sb(tag + "_S", [128, 3, 16, 128])
    qT = [kb.sb(tag + "_qT%d" % i, [128, TP]) for i in range(2)]
    wq = [kb.sb(tag + "_wq%d" % i, [128, DCH, 128]) for i in range(2)]
    wqv = prm['wq'].ap().rearrange("(c p) n -> p c n", p=128)
    thr = kb.sb(tag + "_thr", [128, 3, 8])
    ncc = kb.sb(tag + "_ncc", [128, 3, 8])
    t0a = kb.sb(tag + "_t0a", [128, 16])
    t1a = kb.sb(tag + "_t1a", [128, 16])
    tmp128 = kb.sb(tag + "_tmp128", [128, 128])
    cand = kb.sb(tag + "_cand", [128, 16, 16])
    cand2 = kb.sb(tag + "_cand2", [128, 256])
    best = kb.sb(tag + "_best", [128, 24])
    eb = kb.sb(tag + "_eb", [128, 16])
    zs = kb.sb(tag + "_zsum", [128, 1])
    nmx = kb.sb(tag + "_nmx", [128, 1])
    BF = mybir.dt.bfloat16
    UTt = [kb.sb(tag + "_UT%d" % i, [128, DCH, KG * 128], BF) for i in range(3)]
    Vt = [kb.sb(tag + "_V%d" % i, [128, KG, D], BF) for i in range(3)]
    h2b = kb.sb(tag + "_h2b", [128, DCH, TP], BF)
    UTv = prm['UT'].ap().rearrange("(c p) e -> p c e", p=128)
    Vv = prm['V'].ap().rearrange("(k q) d -> q k d", q=128)
    G = [[kb.sb(tag + "_G%d_%d" % (b_, i), [128, KG * 128]) for i in range(3)] for b_ in range(3)]
    tgs = [kb.sb(tag + "_tg%d" % i, [128, KG, 128]) for i in range(4)]
    egs = [kb.sb(tag + "_eg%d" % i, [128, KG, 128]) for i in range(4)]
    gti = 0
    gA = [kb.sb(tag + "_gA%d" % i, [128, TP]) for i in range(2)]
    GAT = [kb.sb(tag + "_GAT%d" % i, [128, KG, TP], BF) for i in range(2)]
    facc = kb.sb(tag + "_facc", [128, 3, D])
    xr = h2
    pA = [kb.ps(tag + "_pA%d" % i, [128, 512]) for i in range(2)]
    pT = [kb.ps(tag + "_pT%d" % i, [128, 512]) for i in range(2)]
    pF = [kb.ps(tag + "_pF%d" % i, [128, 512]) for i in range(2)]
    pS = [kb.ps(tag + "_pS%d" % i, [128, 512]) for i in range(2)]
    ia = it_ = if_ = is_ = 0
    gi = 0
    for tp in range(npass):
        tk0 = tp * TP
        kb.dma('sp', h2[:, :, :], h2v[:, :, tk0:tk0 + TP])
        for hp in range(16):
            w = wq[hp % 2]
            kb.dma('act', w[:, :, :], wqv[:, :, hp * 128:(hp + 1) * 128])
            p = pA[ia % 2]
            ia += 1
            for c in range(DCH):
                kb.mm(p[:, :TP], w[:, c, :], h2[:, c, :], start=(c == 0), stop=(c == DCH - 1))
            q = qT[hp % 2]
            kb.copy(q[:, :], p[:, :TP], e='act')
            for ti in range(3):
                ps_ = pS[is_ % 2]
                is_ += 1
                kb.mm(ps_[:, :128], q[:, ti * 128:(ti + 1) * 128], keysT[:, hp, :])
                kb.copy(S[:, ti, hp, :], ps_[:, :128], e='dve' if ti % 2 else 'act')
        for ti in range(3):
            for h in range(8):
                for side, ta in ((0, t0a), (1, t1a)):
                    s_ = S[:, ti, 2 * h + side, :]
                    kb.op('dve', lambda eng, o=ta, i=s_: eng.max(out=o[:, 0:8], in_=i), r=[S], w=[ta])
                    kb.op('dve', lambda eng, o=tmp128, m=ta, i=s_: eng.match_replace(out=o[:, :], in_to_replace=m[:, 0:8], in_values=i, imm_value=-1e30),
                          r=[S, ta], w=[tmp128])
                    kb.op('dve', lambda eng, o=ta, i=tmp128: eng.max(out=o[:, 8:16], in_=i[:, :]), r=[tmp128], w=[ta])
                kb.tt(cand[:, :, :], t0a[:, :].unsqueeze(2).to_broadcast([128, 16, 16]), t1a[:, :].unsqueeze(1).to_broadcast([128, 16, 16]), ALU.add)
                cf = cand[:, :, :].rearrange("p a b -> p (a b)")
                kb.op('dve', lambda eng, o=best, i=cf: eng.max(out=o[:, 0:8], in_=i), r=[cand], w=[best])
                kb.op('dve', lambda eng, o=cand2, m=best, i=cf: eng.match_replace(out=o[:, :], in_to_replace=m[:, 0:8], in_values=i, imm_value=-1e30), r=[cand, best], w=[cand2])
                kb.op('dve', lambda eng, o=best, i=cand2: eng.max(out=o[:, 8:16], in_=i[:, :]), r=[cand2], w=[best])
                kb.op('dve', lambda eng, o=thr[:, ti, h:h + 1], i=best: eng.tensor_reduce(out=o, in_=i[:, 0:16], op=ALU.min, axis=mybir.AxisListType.X), r=[best], w=[thr])
                kb.ts(nmx[:, :], best[:, 0:1], -1.0, ALU.mult)
                kb.act(eb[:, :], best[:, 0:16], AF.Exp, bias=nmx[:, 0:1])
                kb.op('dve', lambda eng, o=zs, i=eb: eng.reduce_sum(o[:, :], i[:, :], axis=mybir.AxisListType.X), r=[eb], w=[zs])
                kb.act(zs[:, :], zs[:, :], AF.Ln)
                kb.tt(zs[:, :], zs[:, :], best[:, 0:1], ALU.add)
                kb.ts(ncc[:, ti, h:h + 1], zs[:, :], -1.0, ALU.mult)
        kb.copy(h2b[:, :, :], h2[:, :, :], e='act')
        for ti in range(3):
            kb.memset(facc[:, ti, :], 0.0)
        def stage1(grp):
            nonlocal gti
            for ti in range(3):
                gv = G[grp % 3][ti][:, :].rearrange("p (k q) -> p k q", q=128)
                for h in range(8):
                    s0 = S[:, ti, 2 * h, grp * KG:(grp + 1) * KG].unsqueeze(2).to_broadcast([128, KG, 128])
                    s1 = S[:, ti, 2 * h + 1, :].unsqueeze(1).to_broadcast([128, KG, 128])
                    tg = tgs[gti % 4]
                    eg = egs[gti % 4]
                    gti += 1
                    kb.tt(tg[:, :, :], s0, s1, ALU.add)
                    kb.act(eg[:, :, :], tg[:, :, :], AF.Exp, bias=ncc[:, ti, h:h + 1])
                    if h == 0:
                        kb.stt(gv, tg[:, :, :], thr[:, ti, h:h + 1], eg[:, :, :], ALU.is_ge, ALU.mult)
                    else:
                        kb.stt(eg[:, :, :], tg[:, :, :], thr[:, ti, h:h + 1], eg[:, :, :], ALU.is_ge, ALU.mult)
                        kb.tt(gv, gv, eg[:, :, :], ALU.add, e='pool')
            ut = UTt[grp % 3]
            e0 = grp * KG * 128
            kb.dma('pool', ut[:, :, :], UTv[:, :, e0:e0 + KG * 128])

        def stage2(grp):
            nonlocal ia, it_
            ut = UTt[grp % 3]
            vt = Vt[grp % 3]
            kb.dma('pool', vt[:, :, :], Vv[:, grp * KG:(grp + 1) * KG, :])
            gat = GAT[grp % 2]
            for j in range(KG):
                p = pA[ia % 2]
                ia += 1
                for c in range(DCH):
                    kb.mm(p[:, :TP], ut[:, c, j * 128:(j + 1) * 128], h2b[:, c, :], start=(c == 0), stop=(c == DCH - 1))
                ga = gA[j % 2]
                kb.act(ga[:, :], p[:, :TP], AF.Gelu)
                pt = pT[it_ % 2]
                it_ += 1
                for ti in range(3):
                    kb.transpose(pt[:, ti * 128:(ti + 1) * 128], G[grp % 3][ti][:, j * 128:(j + 1) * 128], ident[:, :])
                kb.tt(gat[:, j, :], ga[:, :], pt[:, :TP], ALU.mult)

        def stage3(grp):
            nonlocal if_
            vt = Vt[grp % 3]
            gat = GAT[grp % 2]
            for ti in range(3):
                for dq in range(4):
                    pf = pF[if_ % 2]
                    if_ += 1
                    for j in range(KG):
                        kb.mm(pf[:, :], gat[:, j, ti * 128:(ti + 1) * 128], vt[:, j, dq * 512:(dq + 1) * 512], start=(j == 0), stop=(j == KG - 1))
                    kb.tt(facc[:, ti, dq * 512:(dq + 1) * 512], facc[:, ti, dq * 512:(dq + 1) * 512], pf[:, :], ALU.add)
        for it in range(NGRP + 2):
            if it < NGRP:
                stage1(it)
            if 0 <= it - 1 < NGRP:
                stage2(it - 1)
            if 0 <= it - 2 < NGRP:
                stage3(it - 2)
        kb.dma('sp', xr[:, :, :], x1v[:, :, tk0:tk0 + TP])
        for c in range(DCH):
            pt = pT[it_ % 2]
            it_ += 1
            for ti in range(3):
                kb.transpose(pt[:, ti * 128:(ti + 1) * 128], facc[:, ti, c * 128:(c + 1) * 128], ident[:, :])
            for ti in range(3):
                seg = 0 if (tp * 3 + ti) < ctx_tiles else 1
                kb.stt(xr[:, c, ti * 128:(ti + 1) * 128], pt[:, ti * 128:(ti + 1) * 128], M['g2'][:, c, seg:seg + 1],
                       xr[:, c, ti * 128:(ti + 1) * 128], ALU.mult, ALU.add)
        if final_nw is not None:
            p = pA[ia % 2]
            ia += 1
            sqf = gA[0]
            for c in range(DCH):
                s_ = gA[c % 2]
                kb.act(s_[:, :], xr[:, c, :], AF.Square)
                kb.mm(p[:, :TP], ones[:, :], s_[:, :], start=(c == 0), stop=(c == DCH - 1))
            rs = qT[0]
            kb.ts(rs[:, :], p[:, :TP], 1.0 / D, ALU.mult, EPS, ALU.add)
            kb.act(rs[:, :], rs[:, :], AF.Sqrt)
            kb.recip(rs[:, :], rs[:, :])
            for c in range(DCH):
                kb.stt(xr[:, c, :], xr[:, c, :], final_nw[:, c:c + 1], rs[:, :], ALU.mult, ALU.mult)
        kb.dma('sp', xov[:, :, tk0:tk0 + TP], xr[:, :, :])
    kb.pop()


NCORES = 8
RW_COLS_, NA_OFF, MB_OFF = 1920, 1920, 3456


def _pc(v):
    return np.ascontiguousarray(np.asarray(v, np.float32).reshape(16, 128).T)


def _rw_host(V, l, hh):
    idx = np.concatenate([hh * 256 + np.arange(256), 512 + hh * 256 + np.arange(256), 1024 + hh * 256 + np.arange(256), np.arange(1536, 1920)])
    my = slice(hh * 256, hh * 256 + 256)

    def col2(v):
        return np.ascontiguousarray(v[my].reshape(2, 128).T).astype(np.float32)
    prm = {}
    prm['mu'] = np.ascontiguousarray(V['rw_mu'][l][:, idx].reshape(2, 9, 128).transpose(2, 1, 0))
    prm['w0'] = np.ascontiguousarray(V['rw_w0'][l][:, my].reshape(2, 2, 128).transpose(2, 1, 0))
    prm['a0'] = np.ascontiguousarray(V['rw_a0'][l][:, my].reshape(2, 2, 128).transpose(2, 1, 0))
    prm['kk'] = col2(V['rw_kk'][l])
    prm['ka'] = col2(V['rw_ka'][l])
    prm['rk'] = col2(V['rw_rk'][l].reshape(512))
    prm['lnw'] = col2(V['rw_ln_w'][l])
    prm['lnb'] = col2(V['rw_ln_b'][l])
    prm['w2'] = np.ascontiguousarray(V['rw_w2'][l][:, :, my].reshape(128, 256))
    prm['a2'] = np.ascontiguousarray(V['rw_a2'][l][:, :, my].reshape(128, 256))
    prm['g2'] = np.ascontiguousarray(V['rw_g2'][l][:, my])
    return idx, prm


def _mb_host(V, l, hh):
    fm = np.concatenate([4480 + hh * 512 + np.arange(512), 5504 + hh * 256 + np.arange(256), 6016 + hh * 256 + np.arange(256)])
    tm = np.concatenate([3456 + hh * 512 + np.arange(512), 6528 + hh * 8 + np.arange(8), 6544 + hh * 8 + np.arange(8)])
    ch = fm - 4480
    prm = {}
    prm['cw'] = np.ascontiguousarray(V['mb_conv_w'][l][:, ch].reshape(5, 8, 128).transpose(2, 1, 0))
    prm['cb'] = np.ascontiguousarray(V['mb_conv_b'][l][ch].reshape(8, 128).T)
    prm['dtb'] = np.ascontiguousarray(V['mb_dt_bias'][l][:, hh * 8: hh * 8 + 8].reshape(16))
    prm['alog'] = np.ascontiguousarray(V['mb_a_log'][l][:, hh * 8: hh * 8 + 8].reshape(16))
    prm['dsk'] = np.ascontiguousarray(V['mb_d'][l][hh * 8: hh * 8 + 8])
    prm['nw'] = np.ascontiguousarray(V['mb_norm_w'][l][hh * 512: hh * 512 + 512])
    return fm, tm, prm


NFM, NTM = 2688, 784
_P1_SHAPES = dict(xT=[2048, 2304], nw1=[128, 16], sc1=[128, 16, 2], sh1=[128, 16, 2], w=[2048, NFM + NTM],
                  rw_mu=[128, 9, 2], rw_w0=[128, 2, 2], rw_a0=[128, 2, 2], rw_kk=[128, 2], rw_ka=[128, 2], rw_rk=[128, 2],
                  rw_lnw=[128, 2], rw_lnb=[128, 2], rw_w2=[128, 256], rw_a2=[128, 256], rw_g2=[128, 256],
                  na_bias=[64, 4, 15, 64], na_mask=[64, 64],
                  mb_cw=[128, 8, 5], mb_cb=[128, 8], mb_dtb=[16], mb_alog=[16], mb_dsk=[8], mb_nw=[512],
                  c_bones=[128, 128], c_ident2=[128, 64], c_ident=[128, 128], c_pmT=[128, 128], c_cos=[128, 2048],
                  c_sin=[128, 2048], c_tri=[2, 128, 128], c_nmask=[2, 128, 128])


def build_p1():
    kb = KB()
    d = {k: kb.dram("i_" + k, s, kind="ExternalInput") for k, s in _P1_SHAPES.items()}
    yT_rw = kb.dram("yT_rw", [256, 2304], kind="ExternalOutput")
    ytok = kb.dram("ytok", [2304, 768], kind="ExternalOutput")
    hT_d = kb.dram("hT_s", [2048, 2304], mybir.dt.bfloat16)
    zT_d = kb.dram("zT_s", [NFM, 2304])
    zk_d = kb.dram("ztok_s", [2304, NTM])
    rws_d = kb.dram("rws_s", [NARR, 128, 2304])
    nw_sb = kb.sb("nw_sb", [128, 16])
    sc_sb = kb.sb("sc_sb", [128, 16, 2])
    sh_sb = kb.sb("sh_sb", [128, 16, 2])
    ones = kb.sb("ones", [128, 128])
    bones = kb.sb("bones", [128, 128])
    ident2 = kb.sb("ident2", [128, 64])
    ident = kb.sb("ident", [128, 128])
    kb.dma('sp', nw_sb[:, :], d['nw1'].ap())
    kb.dma('sp', sc_sb[:, :, :], d['sc1'].ap())
    kb.dma('sp', sh_sb[:, :, :], d['sh1'].ap())
    kb.dma('sp', bones[:, :], d['c_bones'].ap())
    kb.dma('sp', ident2[:, :], d['c_ident2'].ap())
    kb.dma('sp', ident[:, :], d['c_ident'].ap())
    kb.memset(ones[:, :], 1.0)
    kb.push()
    norm_mod_T(kb, d['xT'], hT_d, nw_sb, sc_sb, sh_sb, ones, [0] + [1] * 8, "n1")
    kb.pop()
    kb.push()
    in_proj(kb, hT_d, d['w'], NFM, zT_d, NTM, zk_d, 2304, "ip")
    kb.pop()
    kb.push()
    natten(kb, zT_d, 1152, zk_d, 0, d['na_bias'], d['na_mask'], ytok, 0, "na")
    kb.pop()
    kb.push()
    mamba(kb, zT_d, 1664, zk_d, 256, {k[3:]: v for k, v in d.items() if k.startswith("mb_")},
          dict(ident=ident, pmT=d['c_pmT'], cos=d['c_cos'], sin=d['c_sin'], tri=d['c_tri'], nmask=d['c_nmask']), ytok, 256, "mb")
    kb.pop()
    rwkv(kb, zT_d, 0, {k[3:]: v for k, v in d.items() if k.startswith("rw_")}, rws_d, yT_rw, 0, "rw", dict(bones=bones, ident2=ident2))
    return kb.build()


_P2_SHAPES = dict(yT=[2048, 1152], xT=[2048, 1152], wout=[2048, 2048], g1=[128, 16, 2], sh2=[128, 16, 2], sc2=[128, 16, 2],
                  g2=[128, 16, 2], nw2=[128, 16], wq=[2048, 2048], keysT=[128, 16, 128], UT=[2048, 16384], V=[16384, 2048],
                  c_ident=[128, 128], fnw=[128, 16])


def build_p2(final):
    kb = KB()
    d = {k: kb.dram("i_" + k, s, kind="ExternalInput") for k, s in _P2_SHAPES.items()}
    xo = kb.dram("xo", [2048, 1152], kind="ExternalOutput")
    x1T = kb.dram("x1T_s", [2048, 1152])
    h2T = kb.dram("h2T_s", [2048, 1152])
    ones = kb.sb("ones", [128, 128])
    ident = kb.sb("ident", [128, 128])
    fnw = kb.sb("fnw", [128, 16])
    kb.memset(ones[:, :], 1.0)
    kb.dma('sp', ident[:, :], d['c_ident'].ap())
    kb.dma('sp', fnw[:, :], d['fnw'].ap())
    p2(kb, d['yT'], d['xT'], d, dict(ones=ones, ident=ident), xo, x1T, h2T, "p2", final_nw=fnw if final else None)
    return kb.build()


def build_p0():
    kb = KB()
    cT = kb.dram("cT", [128, 16, 5], kind="ExternalInput")
    aw = kb.dram("aw", [4, 2048, 1536], kind="ExternalInput")
    ab = kb.dram("ab", [128, 4, 12], kind="ExternalInput")
    mo = kb.dram("mo", [128, 4, 12, 5], kind="ExternalOutput")
    sc = kb.sb("sc", [128, 16, 5])
    abs_ = kb.sb("abs", [128, 4, 12])
    out = kb.sb("out", [128, 4, 12, 5])
    kb.dma('sp', sc[:, :, :], cT.ap())
    kb.dma('sp', abs_[:, :, :], ab.ap())
    kb.act(sc[:, :, :], sc[:, :, :], AF.Silu)
    wt = [kb.sb("aw%d" % i, [128, 16, 128]) for i in range(3)]
    pp = [kb.ps("pp%d" % i, [128, 8]) for i in range(2)]
    n = 0
    for l in range(4):
        wv = aw.ap()[l].rearrange("(c p) n -> p c n", p=128)
        for cc in range(12):
            w = wt[n % 3]
            p = pp[n % 2]
            kb.dma('sp' if n % 2 else 'act', w[:, :, :], wv[:, :, cc * 128:(cc + 1) * 128])
            for c in range(16):
                kb.mm(p[:, 0:5], w[:, c, :], sc[:, c, :], start=(c == 0), stop=(c == 15))
            kb.ts(out[:, l, cc, :], p[:, 0:5], abs_[:, l, cc:cc + 1], ALU.add)
            n += 1
    kb.dma('sp', mo.ap(), out[:, :, :, :])
    return kb.build()


_NC_CACHE = {}
_NLAYERS = 4
_DBG = {}


def _get(name, fn):
    if name not in _NC_CACHE:
        _NC_CACHE[name] = fn()
    return _NC_CACHE[name]


def kernel(**inp):
    V = {k: np.asarray(v) for k, v in inp.items()}
    cores = list(range(NCORES))
    c_all = np.concatenate([V['c'], V['c_ctx'][None, :]], 0).astype(np.float32)
    cT = np.ascontiguousarray(c_all.T.reshape(16, 128, 5).transpose(1, 0, 2))
    ins = []
    for c in cores:
        cs = slice(c * 1536, (c + 1) * 1536)
        ins.append(dict(cT=cT, aw=np.ascontiguousarray(V['ada_w'][:, :, cs]),
                        ab=np.ascontiguousarray(V['ada_b'][:, cs].reshape(4, 12, 128).transpose(2, 0, 1))))
    res = run_bass_kernel_spmd(_get('p0', build_p0), ins, core_ids=cores)
    mods = np.zeros((4, 12288, 5), np.float32)
    for c in cores:
        mo = res.results[c]['mo']
        mods[:, c * 1536:(c + 1) * 1536, :] = mo.transpose(1, 2, 0, 3).reshape(4, 1536, 5)

    def seg2(l, j, b):
        v = mods[l, j * 2048:(j + 1) * 2048, :]
        return np.ascontiguousarray(np.stack([_pc(v[:, 4]), _pc(v[:, b])], -1))
    consts = mb_consts()
    cst = dict(c_bones=np.kron(np.eye(2), np.ones((64, 64))).astype(np.float32),
               c_ident2=np.concatenate([np.eye(64), np.eye(64)], 0).astype(np.float32),
               c_ident=np.eye(128, dtype=np.float32))
    cst.update({"c_" + k: v for k, v in consts.items()})
    xs = []
    for c in cores:
        b, th = c // 2, c % 2
        x = np.concatenate([V['ctx'][b][th * 128:(th + 1) * 128], V['x'][b][th * 1024:(th + 1) * 1024]], 0)
        xs.append(np.ascontiguousarray(x.T))
    for l in range(_NLAYERS):
        ins = []
        for c in cores:
            b, hh = c // 2, c % 2
            s0, s1 = xs[2 * b], xs[2 * b + 1]
            xT = np.ascontiguousarray(np.concatenate([s0[:, :128], s1[:, :128], s0[:, 128:], s1[:, 128:]], 1))
            idx, rwp = _rw_host(V, l, hh)
            fm, tm, mbp = _mb_host(V, l, hh)
            na_q = NA_OFF + hh * 256 + np.arange(256)
            cols = np.concatenate([idx, na_q, na_q + 512, fm, na_q + 1024, tm])
            g, mask = na_host_tables(V['na_rpb'][l], slice(hh * 4, hh * 4 + 4))
            dct = dict(xT=xT, nw1=_pc(V['norm1_w'][l]), sc1=seg2(l, 1, b), sh1=seg2(l, 0, b),
                       w=np.ascontiguousarray(V['w_in'][l][:, cols]), na_bias=g, na_mask=mask)
            dct.update({"rw_" + k: v for k, v in rwp.items()})
            dct.update({"mb_" + k: v for k, v in mbp.items()})
            dct.update(cst)
            ins.append({"i_" + k: v for k, v in dct.items()})
        res = run_bass_kernel_spmd(_get('p1', build_p1), ins, core_ids=cores)
        UT = np.ascontiguousarray(V['pe_u'][l].T)
        keysT = np.ascontiguousarray(V['pe_keys'][l].reshape(16, 128, 128).transpose(2, 0, 1))
        ins = []
        for c in cores:
            b, th = c // 2, c % 2
            tok = np.concatenate([th * 128 + np.arange(128), 256 + th * 1024 + np.arange(1024)])
            parts = []
            r0, r1 = res.results[2 * b], res.results[2 * b + 1]
            yT = np.concatenate([r0['yT_rw'][:, tok], r1['yT_rw'][:, tok],
                                 r0['ytok'][tok, 0:256].T, r1['ytok'][tok, 0:256].T,
                                 r0['ytok'][tok, 256:768].T, r1['ytok'][tok, 256:768].T], 0)
            dct = dict(yT=np.ascontiguousarray(yT), xT=xs[c], wout=V['w_out'][l], g1=seg2(l, 2, b), sh2=seg2(l, 3, b),
                       sc2=seg2(l, 4, b), g2=seg2(l, 5, b), nw2=_pc(V['norm2_w'][l]), wq=V['pe_wq'][l], keysT=keysT,
                       UT=UT, V=V['pe_v'][l], c_ident=cst['c_ident'], fnw=_pc(V['final_norm_w']))
            ins.append({"i_" + k: v for k, v in dct.items()})
        final = (l == 3)
        res2 = run_bass_kernel_spmd(_get('p2f' if final else 'p2', lambda: build_p2(final)), ins, core_ids=cores)
        xs = [np.ascontiguousarray(res2.results[c]['xo']) for c in cores]
        _DBG['xs%d' % l] = xs
    out = np.zeros((4, 2048, 2048), np.float32)
    for c in cores:
        b, th = c // 2, c % 2
        out[b, th * 1024:(th + 1) * 1024, :] = xs[c][:, 128:].T
    return out

_RUN_KW = {}


class APW:
    def __init__(self, ap):
        self._ap = ap

    def ap(self):
        return self._ap


_P1_LH = ['w', 'rw_mu', 'rw_w0', 'rw_a0', 'rw_kk', 'rw_ka', 'rw_rk', 'rw_lnw', 'rw_lnb', 'rw_w2', 'rw_a2', 'rw_g2',
          'na_bias', 'mb_cw', 'mb_cb', 'mb_dtb', 'mb_alog', 'mb_dsk', 'mb_nw']
_P1_CONST = ['na_mask', 'c_bones', 'c_ident2', 'c_ident', 'c_pmT', 'c_cos', 'c_sin', 'c_tri', 'c_nmask']
_P2_L = ['wout', 'nw2', 'wq', 'keysT', 'UT', 'V']
FUSED_HALVES = [(0, 1152, [(0, 256, 0), (256, 512, 1), (768, 384, 1)]), (1152, 1152, [(0, 512, 1), (512, 512, 1), (1024, 128, 1)])]


def y_transpose(kb, ytok_d, yT_d, hh, ident, tag):
    yt = [kb.sb(tag + "_yt%d" % i, [128, 768]) for i in range(2)]
    yo = [kb.sb(tag + "_yo%d" % i, [128, 6, 128]) for i in range(2)]
    pt = [kb.ps(tag + "_pt%d" % i, [128, 512]) for i in range(4)]
    rows = [512 + hh * 256, 512 + hh * 256 + 128] + [1024 + hh * 512 + j * 128 for j in range(4)]
    for ti in range(18):
        y = yt[ti % 2]
        o = yo[ti % 2]
        kb.dma('sp', y[:, :], ytok_d.ap()[ti * 128:(ti + 1) * 128, :])
        for j in range(6):
            p = pt[(ti * 6 + j) % 4]
            kb.transpose(p[:, 0:128], y[:, j * 128:(j + 1) * 128], ident[:, :])
            kb.copy(o[:, j, :], p[:, 0:128], e='act' if j % 2 else 'dve')
        for j in range(6):
            kb.dma('act' if j % 2 else 'sp', yT_d.ap()[rows[j]:rows[j] + 128, ti * 128:(ti + 1) * 128], o[:, j, :])


def build_fused(nlayers=4):
    kb = KB()
    d = {}
    d['xT'] = kb.dram("i_xT", [2048, 2304], kind="ExternalInput")
    d['cT'] = kb.dram("i_cT", [128, 16, 2], kind="ExternalInput")
    d['aw'] = kb.dram("i_aw", [4, 2048, 12288], kind="ExternalInput")
    d['ab'] = kb.dram("i_ab", [128, 4, 96], kind="ExternalInput")
    d['nw1'] = kb.dram("i_nw1", [4, 128, 16], kind="ExternalInput")
    d['fnw'] = kb.dram("i_fnw", [128, 16], kind="ExternalInput")
    for k in _P1_LH:
        d[k] = kb.dram("i_" + k, [4, 2] + _P1_SHAPES[k], kind="ExternalInput")
    for k in _P1_CONST:
        d[k] = kb.dram("i_" + k, _P1_SHAPES[k], kind="ExternalInput")
    for k in _P2_L:
        d[k] = kb.dram("i_" + k, [4] + _P2_SHAPES[k], kind="ExternalInput")
    out_d = kb.dram("o_xT", [2048, 2304], kind="ExternalOutput")
    xT_s = kb.dram("xT_s", [2048, 2304])
    yT_s = kb.dram("yT_s", [2048, 2304])
    hT_s = kb.dram("hT_s", [2048, 2304], mybir.dt.bfloat16)
    zT_s = kb.dram("zT_s", [NFM, 2304])
    zk_s = kb.dram("ztok_s", [2304, NTM])
    rws_s = kb.dram("rws_s", [NARR, 128, 2304])
    ytok_s = kb.dram("ytok_s", [2304, 768])
    x1T_s = kb.dram("x1T_s", [2048, 2304])
    h2T_s = kb.dram("h2T_s", [2048, 2304])
    mods_s = kb.dram("mods_s", [4, 6, 128, 16, 2])
    ones = kb.sb("ones", [128, 128])
    bones = kb.sb("bones", [128, 128])
    ident2 = kb.sb("ident2", [128, 64])
    ident = kb.sb("ident", [128, 128])
    fnw = kb.sb("fnw", [128, 16])
    kb.memset(ones[:, :], 1.0)
    kb.dma('sp', bones[:, :], d['c_bones'].ap())
    kb.dma('sp', ident2[:, :], d['c_ident2'].ap())
    kb.dma('sp', ident[:, :], d['c_ident'].ap())
    kb.dma('sp', fnw[:, :], d['fnw'].ap())
    kb.dma('sp', xT_s.ap(), d['xT'].ap())
    kb.push()
    sc = kb.sb("p0_sc", [128, 16, 2])
    abs_ = kb.sb("p0_ab", [128, 4, 96])
    mo = kb.sb("p0_mo", [128, 4, 6, 16, 2])
    kb.dma('sp', sc[:, :, :], d['cT'].ap())
    kb.dma('sp', abs_[:, :, :], d['ab'].ap())
    kb.act(sc[:, :, :], sc[:, :, :], AF.Silu)
    wt = [kb.sb("p0_w%d" % i, [128, 16, 512]) for i in range(2)]
    pp = [kb.ps("p0_pp%d" % i, [128, 8]) for i in range(2)]
    n = 0
    for l in range(nlayers):
        wv = d['aw'].ap()[l].rearrange("(c p) n -> p c n", p=128)
        for cg in range(24):
            w = wt[cg % 2]
            kb.dma('sp' if cg % 2 else 'act', w[:, :, :], wv[:, :, cg * 512:(cg + 1) * 512])
            for q in range(4):
                cc = cg * 4 + q
                p = pp[n % 2]
                n += 1
                for c in range(16):
                    kb.mm(p[:, 0:2], w[:, c, q * 128:(q + 1) * 128], sc[:, c, :], start=(c == 0), stop=(c == 15))
                kb.ts(mo[:, l, cc // 16, cc % 16, :], p[:, 0:2], abs_[:, l, cc:cc + 1], ALU.add)
    for l in range(nlayers):
        for j in range(6):
            kb.dma('sp', mods_s.ap()[l, j], mo[:, l, j, :, :])
    kb.pop()
    nw_sb = kb.sb("nw_sb", [128, 16])
    sc_sb = kb.sb("sc_sb", [128, 16, 2])
    sh_sb = kb.sb("sh_sb", [128, 16, 2])
    for l in range(nlayers):
        kb.dma('sp', nw_sb[:, :], d['nw1'].ap()[l])
        kb.dma('sp', sc_sb[:, :, :], mods_s.ap()[l, 1])
        kb.dma('sp', sh_sb[:, :, :], mods_s.ap()[l, 0])
        kb.push()
        norm_mod_T(kb, xT_s, hT_s, nw_sb, sc_sb, sh_sb, ones, [0] + [1] * 8, "n1_%d" % l)
        kb.pop()
        for hh in range(2):
            t = "L%dh%d" % (l, hh)
            g = lambda k: APW(d[k].ap()[l, hh])
            kb.push()
            in_proj(kb, hT_s, g('w'), NFM, zT_s, NTM, zk_s, 2304, "ip" + t)
            kb.pop()
            kb.push()
            natten(kb, zT_s, 1152, zk_s, 0, g('na_bias'), d['na_mask'], ytok_s, 0, "na" + t)
            kb.pop()
            kb.push()
            mamba(kb, zT_s, 1664, zk_s, 256, {k[3:]: g(k) for k in _P1_LH if k.startswith("mb_")},
                  dict(ident=ident, pmT=d['c_pmT'], cos=d['c_cos'], sin=d['c_sin'], tri=d['c_tri'], nmask=d['c_nmask']), ytok_s, 256, "mb" + t)
            kb.pop()
            rwkv(kb, zT_s, 0, {k[3:]: g(k) for k in _P1_LH if k.startswith("rw_")}, rws_s, yT_s, hh * 256, "rw" + t,
                 dict(bones=bones, ident2=ident2))
            kb.push()
            y_transpose(kb, ytok_s, yT_s, hh, ident, "yt" + t)
            kb.pop()
        final = (l == nlayers - 1)
        prm = {k: APW(d[k].ap()[l]) for k in _P2_L}
        for j, k in ((2, 'g1'), (3, 'sh2'), (4, 'sc2'), (5, 'g2')):
            prm[k] = APW(mods_s.ap()[l, j])
        p2(kb, yT_s, xT_s, prm, dict(ones=ones, ident=ident), out_d if final else xT_s, x1T_s, h2T_s, "p2L%d" % l,
           final_nw=fnw if final else None, ntok=2304, halves=FUSED_HALVES, ctx_tiles=2)
    return kb.build()


def _fused_inputs(V, b, nlayers=4):
    ins = {}
    x = np.concatenate([V['ctx'][b], V['x'][b]], 0)
    ins['xT'] = np.ascontiguousarray(x.T)
    c2 = np.stack([V['c_ctx'], V['c'][b]], 0).astype(np.float32)
    ins['cT'] = np.ascontiguousarray(c2.T.reshape(16, 128, 2).transpose(1, 0, 2))
    ins['aw'] = V['ada_w']
    ins['ab'] = np.ascontiguousarray(V['ada_b'].reshape(4, 96, 128).transpose(2, 0, 1))
    ins['nw1'] = np.stack([_pc(V['norm1_w'][l]) for l in range(4)])
    ins['fnw'] = _pc(V['final_norm_w'])
    return ins


def kernel_fused(V, ncores=8, nlayers=4):
    consts = mb_consts()
    cst = dict(c_bones=np.kron(np.eye(2), np.ones((64, 64))).astype(np.float32),
               c_ident2=np.concatenate([np.eye(64), np.eye(64)], 0).astype(np.float32),
               c_ident=np.eye(128, dtype=np.float32))
    cst.update({"c_" + k: v for k, v in consts.items()})
    shared = {}
    lh = {k: [] for k in _P1_LH}
    for l in range(4):
        row = {k: [] for k in _P1_LH}
        for hh in range(2):
            idx, rwp = _rw_host(V, l, hh)
            fm, tm, mbp = _mb_host(V, l, hh)
            na_q = NA_OFF + hh * 256 + np.arange(256)
            cols = np.concatenate([idx, na_q, na_q + 512, fm, na_q + 1024, tm])
            g, mask = na_host_tables(V['na_rpb'][l], slice(hh * 4, hh * 4 + 4))
            dct = dict(w=V['w_in'][l][:, cols], na_bias=g)
            dct.update({"rw_" + k: v for k, v in rwp.items()})
            dct.update({"mb_" + k: v for k, v in mbp.items()})
            for k in _P1_LH:
                row[k].append(dct[k])
        for k in _P1_LH:
            lh[k].append(np.stack(row[k]))
    for k in _P1_LH:
        shared[k] = np.ascontiguousarray(np.stack(lh[k]).astype(np.float32))
    shared['na_mask'] = mask
    shared.update(cst)
    shared['wout'] = V['w_out']
    shared['nw2'] = np.stack([_pc(V['norm2_w'][l]) for l in range(4)])
    shared['wq'] = V['pe_wq']
    shared['keysT'] = np.ascontiguousarray(V['pe_keys'].reshape(4, 16, 128, 128).transpose(0, 3, 1, 2))
    shared['UT'] = np.ascontiguousarray(V['pe_u'].transpose(0, 2, 1))
    shared['V'] = V['pe_v']
    ins = []
    for c in range(ncores):
        b = (c // 2) if ncores == 8 else c
        dct = dict(shared)
        dct.update(_fused_inputs(V, b))
        ins.append({"i_" + k: v for k, v in dct.items()})
    res = run_bass_kernel_spmd(_get('fused%d' % nlayers, lambda: build_fused(nlayers)), ins, core_ids=list(range(ncores)), **_RUN_KW)
    out = np.zeros((4, 2048, 2048), np.float32)
    for b in range(4):
        if (2 * b if ncores == 8 else b) >= len(res.results): break
        r = res.results[2 * b if ncores == 8 else b]["o_xT"]
        out[b] = r[:, 256:].T
    _DBG['fused_res'] = res
    return out


_kernel_unfused = kernel


def kernel(**inp):
    V = {k: np.asarray(v) for k, v in inp.items()}
    return kernel_fused(V, ncores=NCORES_FUSED)


NCORES_FUSED = 8
```

```python
import numpy as np
from contextlib import ExitStack
import concourse.bass as bass
import concourse.mybir as mybir
from concourse.bass_utils import run_bass_kernel_spmd

F32 = mybir.dt.float32
AF = mybir.ActivationFunctionType
ALU = mybir.AluOpType


class KB:
    ENG = ['pe', 'dve', 'act', 'pool', 'sp']

    def __init__(self):
        self.nc = bass.Bass("TRN2", target_bir_lowering=False)
        self.es = ExitStack()
        nc = self.nc
        self.gen = {e: 0 for e in self.ENG}
        self.semk = {e: e + "#0" for e in self.ENG}
        self.semh = {e + "#0": self.es.enter_context(nc.semaphore("s_" + e + "_0")) for e in self.ENG}
        self.cnt = {e: 0 for e in self.ENG}
        self.rot_limit = 40000
        self.ndma = 24
        self.dsem = [self.es.enter_context(nc.semaphore("d%d" % i)) for i in range(self.ndma)]
        self.dcnt = [0] * self.ndma
        self.dnext = 0
        self.waited = {e: {} for e in self.ENG}
        self.last_w = {}
        self.readers = {}
        self.prog = {e: [] for e in self.ENG}
        self.ninst = 0
        self.stack = [self.es]

    def sb(self, name, shape, dtype=F32):
        return self.stack[-1].enter_context(self.nc.sbuf_tensor(name, list(shape), dtype))

    def ps(self, name, shape, dtype=F32):
        return self.stack[-1].enter_context(self.nc.psum_tensor(name, list(shape), dtype))

    def push(self):
        self.stack.append(ExitStack())

    def pop(self):
        self.fence()
        self.stack.pop().close()

    def fence(self):
        need = {self.semk[e]: self.cnt[e] for e in self.ENG if self.cnt[e]}
        for i in range(self.ndma):
            if self.dcnt[i]:
                need[('d', i)] = self.dcnt[i]
        for e in self.ENG:
            self._emit_waits(e, dict(need), fence=True)
        for e in self.ENG:
            if self.cnt[e] > self.rot_limit:
                self.gen[e] += 1
                k = "%s#%d" % (e, self.gen[e])
                self.semk[e] = k
                self.semh[k] = self.es.enter_context(self.nc.semaphore("s_%s_%d" % (e, self.gen[e])))
                self.cnt[e] = 0
        self.last_w = {}
        self.readers = {}

    def dram(self, name, shape, dtype=F32, kind="Internal"):
        return self.nc.dram_tensor(name, list(shape), dtype, kind=kind)

    def _key(self, a):
        if isinstance(a, (str, tuple)):
            return a
        if hasattr(a, 'tensor'):
            return a.tensor.name
        return a.name

    def _semh(self, s):
        return self.semh[s] if isinstance(s, str) else self.dsem[s[1]]

    def _deps(self, reads, writes):
        need = {}

        def add(ev):
            if ev is None:
                return
            s, v = ev
            if need.get(s, 0) < v:
                need[s] = v
        for a in reads:
            add(self.last_w.get(self._key(a)))
        for a in writes:
            k = self._key(a)
            add(self.last_w.get(k))
            for s, v in self.readers.get(k, {}).items():
                add((s, v))
        return need

    def _emit_waits(self, e, need, fence=False):
        for s, v in need.items():
            if isinstance(s, str) and s.startswith('pe#') and e == 'pe' and not fence:
                continue
            if self.waited[e].get(s, 0) >= v:
                continue
            self.waited[e][s] = v
            semh = self._semh(s)
            self.prog[e].append(lambda eng, semh=semh, v=v: eng.wait_ge(semh, v))
            self.ninst += 1

    def _record(self, ev, reads, writes):
        wk = [self._key(a) for a in writes]
        for k in wk:
            self.last_w[k] = ev
            self.readers[k] = {}
        for a in reads:
            k = self._key(a)
            if k in wk:
                continue
            r = self.readers.setdefault(k, {})
            if r.get(ev[0], 0) < ev[1]:
                r[ev[0]] = ev[1]

    def op(self, e, fn, r=(), w=()):
        need = self._deps(r, w)
        self._emit_waits(e, need)
        self.cnt[e] += 1
        semh = self.semh[self.semk[e]]
        self.prog[e].append(lambda eng, fn=fn, semh=semh: fn(eng).then_inc(semh, 1))
        self.ninst += 1
        self._record((self.semk[e], self.cnt[e]), r, w)

    def dma(self, e, out, in_, r=None, w=None):
        r = [in_] if r is None else r
        w = [out] if w is None else w
        need = self._deps(r, w)
        i = self.dnext
        self.dnext = (self.dnext + 1) % self.ndma
        if self.dcnt[i]:
            need[('d', i)] = max(need.get(('d', i), 0), self.dcnt[i])
        self._emit_waits(e, need)
        self.dcnt[i] += 16
        semh = self.dsem[i]
        self.prog[e].append(lambda eng, out=out, in_=in_, semh=semh: eng.dma_start(out=out, in_=in_).then_inc(semh, 16))
        self.ninst += 1
        self._record((('d', i), self.dcnt[i]), r, w)

    def mm(self, out, lhsT, rhs, start=True, stop=True, r=None, w=None, fast=False):
        rr = [lhsT, rhs] if r is None else r
        if fast:
            lhsT = lhsT.bitcast(mybir.dt.float32r)
            rhs = rhs.bitcast(mybir.dt.float32r)
        self.op('pe', lambda eng: eng.matmul(out, lhsT, rhs, start=start, stop=stop),
                r=rr, w=[out] if w is None else w)

    def transpose(self, out, in_, ident):
        self.op('pe', lambda eng: eng.transpose(out, in_, ident), r=[in_, ident], w=[out])

    def act(self, out, in_, func, bias=None, scale=None, r=None, w=None, e='act'):
        kw = {}
        rr = [in_]
        if bias is not None:
            kw['bias'] = bias
            if not isinstance(bias, (int, float)):
                rr.append(bias)
        if scale is not None:
            kw['scale'] = scale
            if not isinstance(scale, (int, float)):
                rr.append(scale)
        self.op(e, lambda eng: eng.activation(out, in_, func, **kw), r=rr if r is None else r, w=[out] if w is None else w)

    def tt(self, out, in0, in1, op, e='dve', r=None, w=None):
        self.op(e, lambda eng: eng.tensor_tensor(out, in0, in1, op), r=[in0, in1] if r is None else r, w=[out] if w is None else w)

    def ts(self, out, in0, s1, op0, s2=None, op1=None, e='dve', r=None, w=None):
        rr = [in0]
        for s in (s1, s2):
            if s is not None and not isinstance(s, (int, float)):
                rr.append(s)
        if op1 is None:
            f = lambda eng: eng.tensor_scalar(out, in0, s1, None, op0)
        else:
            f = lambda eng: eng.tensor_scalar(out, in0, s1, s2, op0, op1)
        self.op(e, f, r=rr if r is None else r, w=[out] if w is None else w)

    def stt(self, out, in0, scalar, in1, op0, op1, r=None, w=None):
        rr = [in0, in1]
        if not isinstance(scalar, (int, float)):
            rr.append(scalar)
        self.op('dve', lambda eng: eng.scalar_tensor_tensor(out, in0, scalar, in1, op0, op1),
                r=rr if r is None else r, w=[out] if w is None else w)

    def copy(self, out, in_, e='dve', r=None, w=None):
        if e == 'act':
            f = lambda eng: eng.copy(out, in_)
        else:
            f = lambda eng: eng.tensor_copy(out, in_)
        self.op(e, f, r=[in_] if r is None else r, w=[out] if w is None else w)

    def memset(self, out, val, e='pool'):
        self.op(e, lambda eng: eng.memset(out, val), r=[], w=[out])

    def recip(self, out, in_):
        self.op('dve', lambda eng: eng.reciprocal(out, in_), r=[in_], w=[out])

    def build(self):
        for i in range(self.ndma):
            if self.dcnt[i]:
                semh = self.dsem[i]
                v = self.dcnt[i]
                self.prog['sp'].append(lambda eng, semh=semh, v=v: eng.wait_ge(semh, v))
        nc = self.nc
        with nc.Block() as block:
            for e, reg in (('sp', block.sync), ('pe', block.tensor), ('dve', block.vector),
                           ('act', block.scalar), ('pool', block.gpsimd)):
                prog = self.prog[e]

                def f(eng, prog=prog):
                    for p in prog:
                        p(eng)
                reg(f)
        self.es.close()
        return nc


T = 2304
TB = 256
NBLK = T // TB
D = 2048
DCH = 16
EPS = 1e-6
FAST = False
BF16 = mybir.dt.bfloat16


def norm_mod_T(kb, xT_d, hT_d, nw_sb, sc_sb, sh_sb, ones_sb, seg_of_blk, tag):
    nc = kb.nc
    nseg = sc_sb.shape[2]
    A = kb.sb(tag + "_A", [128, DCH, nseg])
    kb.ts(A[:, :, :], sc_sb[:, :, :], 1.0, ALU.add)
    kb.tt(A[:, :, :], A[:, :, :], nw_sb[:, :].unsqueeze(2).to_broadcast([128, DCH, nseg]), ALU.mult)
    xb = [kb.sb(tag + "_xb%d" % i, [128, DCH, TB]) for i in range(2)]
    hb = [kb.sb(tag + "_hb%d" % i, [128, DCH, TB], BF16) for i in range(2)]
    htmp = [kb.sb(tag + "_htmp%d" % i, [128, TB]) for i in range(2)]
    sq = kb.sb(tag + "_sq", [128, DCH, TB])
    rstd = kb.sb(tag + "_rstd", [128, TB])
    ps = kb.ps(tag + "_ps", [128, TB])
    xv = xT_d.ap().rearrange("(c p) t -> p c t", p=128)
    hv = hT_d.ap().rearrange("(c p) t -> p c t", p=128)
    for blk, seg in enumerate(seg_of_blk):
        x = xb[blk % 2]
        h = hb[blk % 2]
        kb.dma('sp', x[:, :, :], xv[:, :, blk * TB:(blk + 1) * TB])
        kb.act(sq[:, :, :], x[:, :, :], AF.Square)
        for c in range(DCH):
            kb.mm(ps[:, :], ones_sb[:, :], sq[:, c, :], start=(c == 0), stop=(c == DCH - 1))
        kb.ts(rstd[:, :], ps[:, :], 1.0 / D, ALU.mult, EPS, ALU.add)
        kb.act(rstd[:, :], rstd[:, :], AF.Sqrt)
        kb.recip(rstd[:, :], rstd[:, :])
        for c in range(DCH):
            ht = htmp[c % 2]
            kb.tt(ht[:, :], x[:, c, :], rstd[:, :], ALU.mult)
            kb.act(h[:, c, :], ht[:, :], AF.Identity, bias=sh_sb[:, c, seg:seg + 1], scale=A[:, c, seg:seg + 1])
        kb.dma('sp', hv[:, :, blk * TB:(blk + 1) * TB], h[:, :, :])


def in_proj(kb, hT_d, w_d, ncols_fm, zT_d, ncols_tm, ztok_d, ntok, tag):
    hv = hT_d.ap().rearrange("(c p) t -> p c t", p=128)
    wv = w_d.ap().rearrange("(c p) n -> p c n", p=128)
    wt = [kb.sb(tag + "_w%d" % i, [128, DCH, 512], BF16) for i in range(2)]
    hb = [kb.sb(tag + "_ih%d" % i, [128, DCH, TB], BF16) for i in range(3)]
    zs = [kb.sb(tag + "_zs%d" % i, [128, ntok]) for i in range(4)]
    pz = [kb.ps(tag + "_pz%d" % i, [128, 512]) for i in range(2)]
    nblk = ntok // TB
    assert ncols_fm % 128 == 0
    ngrp = (ncols_fm + 511) // 512
    it = 0
    pi = 0
    for g in range(ngrp):
        c0 = g * 512
        nc_ = min(512, ncols_fm - c0)
        w = wt[g % 2]
        kb.dma('pool', w[:, :, :nc_], wv[:, :, c0:c0 + nc_])
        for blk in range(nblk):
            h = hb[it % 3]
            it += 1
            kb.dma('act', h[:, :, :], hv[:, :, blk * TB:(blk + 1) * TB])
            for cc in range(nc_ // 128):
                p = pz[pi % 2]
                pi += 1
                for c in range(DCH):
                    kb.mm(p[:, :TB], w[:, c, cc * 128:(cc + 1) * 128], h[:, c, :], start=(c == 0), stop=(c == DCH - 1), fast=FAST)
                kb.copy(zs[cc][:, blk * TB:(blk + 1) * TB], p[:, :TB], e='act' if cc % 2 else 'dve')
        for cc in range(nc_ // 128):
            kb.dma('sp', zT_d.ap()[c0 + cc * 128:c0 + (cc + 1) * 128, :], zs[cc][:, :])
    ntg = (ncols_tm + 511) // 512
    zt = [kb.sb(tag + "_zt%d" % i, [128, 512]) for i in range(2)]
    zi = 0
    for g in range(ntg):
        c0 = g * 512
        nc_ = min(512, ncols_tm - c0)
        w = wt[(ngrp + g) % 2]
        kb.dma('pool', w[:, :, :nc_], wv[:, :, ncols_fm + c0:ncols_fm + c0 + nc_])
        for blk in range(nblk):
            h = hb[it % 3]
            it += 1
            kb.dma('act', h[:, :, :], hv[:, :, blk * TB:(blk + 1) * TB])
            for tt_ in range(TB // 128):
                p = pz[pi % 2]
                pi += 1
                for c in range(DCH):
                    kb.mm(p[:, :nc_], h[:, c, tt_ * 128:(tt_ + 1) * 128], w[:, c, :nc_], start=(c == 0), stop=(c == DCH - 1), fast=FAST)
                z = zt[zi % 2]
                zi += 1
                kb.copy(z[:, :nc_], p[:, :nc_], e='act' if zi % 2 else 'dve')
                t0 = blk * TB + tt_ * 128
                kb.dma('sp', ztok_d.ap()[t0:t0 + 128, c0:c0 + nc_], z[:, :nc_])


T = 2304
TC = 256
ROWS = 32
GW = 64


def natten(kb, zT_d, qrow0, ztok_d, vcol0, bias_d, mask_d, y_d, ycol0, tag):
    qT = [kb.sb(tag + "_q%d" % i, [128, T]) for i in range(2)]
    kT = [kb.sb(tag + "_k%d" % i, [128, T]) for i in range(2)]
    for i in range(2):
        kb.dma('sp', qT[i][:, :], zT_d.ap()[qrow0 + i * 128: qrow0 + (i + 1) * 128, :])
        kb.dma('act', kT[i][:, :], zT_d.ap()[qrow0 + 256 + i * 128: qrow0 + 256 + (i + 1) * 128, :])
        kb.ts(qT[i][:, :], qT[i][:, :], 0.125, ALU.mult)
    vl = kb.sb(tag + "_vl", [64, ROWS, 4, 65])
    vc = kb.sb(tag + "_vc", [128, 2, 4, 65])
    kb.memset(vl[:, :, :, 64:65], 1.0)
    kb.memset(vc[:, :, :, 64:65], 1.0)
    vsrc = ztok_d.ap()[:, vcol0:vcol0 + 256]
    for hh in range(4):
        kb.dma('sp', vl[:, :, hh, 0:64], vsrc[TC:T, hh * 64:(hh + 1) * 64].rearrange("(r k) d -> k r d", k=64))
        kb.dma('sp', vc[:, :, hh, 0:64], vsrc[0:TC, hh * 64:(hh + 1) * 64].rearrange("(r k) d -> k r d", k=128))
    bias = kb.sb(tag + "_bias", [64, 4, 15, 64])
    mask = kb.sb(tag + "_mask", [64, 64])
    kb.dma('sp', bias[:, :, :, :], bias_d.ap())
    kb.dma('sp', mask[:, :], mask_d.ap())
    kb.tt(bias[:, :, :, :], bias[:, :, :, :], mask[:, :].unsqueeze(1).unsqueeze(1).to_broadcast([64, 4, 15, 64]), ALU.add)
    pw = [kb.ps(tag + "_pw%d" % i, [64, 8, 64]) for i in range(2)]
    pc = [kb.ps(tag + "_pc%d" % i, [128, 2, 64]) for i in range(2)]
    py = [kb.ps(tag + "_py%d" % i, [64, 65]) for i in range(2)]
    ew = [kb.sb(tag + "_ew%d" % i, [64, 8, 64]) for i in range(2)]
    ec = [kb.sb(tag + "_ec%d" % i, [128, 2, 64]) for i in range(2)]
    ys = [kb.sb(tag + "_ys%d" % i, [64, 64]) for i in range(2)]
    rd = [kb.sb(tag + "_rd%d" % i, [64, 1]) for i in range(2)]
    it = 0
    for hh in range(4):
        ci, pb = hh // 2, (hh % 2) * 64
        q_h = qT[ci]
        k_h = kT[ci]
        for qi in range(4 + ROWS):
            i2 = it % 2
            it += 1
            q0 = qi * 64 if qi < 4 else TC + (qi - 4) * 64
            qs = q_h[pb:pb + 64, q0:q0 + 64]
            for kt in range(2):
                kb.mm(pc[i2][:, kt, :], k_h[pb:pb + 64, kt * 128:(kt + 1) * 128], qs)
            kb.act(ec[i2][:, :, :], pc[i2][:, :, :], AF.Exp)
            nk = 2
            if qi >= 4:
                r = qi - 4
                kr0 = min(max(r - 4, 0), ROWS - 8)
                for j in range(8):
                    kk0 = TC + (kr0 + j) * 64
                    kb.mm(pw[i2][:, j, :], k_h[pb:pb + 64, kk0:kk0 + 64], qs)
                dr0 = kr0 - r + 7
                kb.tt(ew[i2][:, :, :], pw[i2][:, :, :], bias[:, hh, dr0:dr0 + 8, :], ALU.add)
                kb.act(ew[i2][:, :, :], ew[i2][:, :, :], AF.Exp)
                nk = 10
            n = 0
            for kt in range(2):
                kb.mm(py[i2][:, :], ec[i2][:, kt, :], vc[:, kt, hh, :], start=(n == 0), stop=(n == nk - 1))
                n += 1
            if qi >= 4:
                for j in range(8):
                    kb.mm(py[i2][:, :], ew[i2][:, j, :], vl[:, kr0 + j, hh, :], start=False, stop=(n == nk - 1))
                    n += 1
            kb.recip(rd[i2][:, :], py[i2][:, 64:65])
            kb.ts(ys[i2][:, :], py[i2][:, 0:64], rd[i2][:, 0:1], ALU.mult)
            kb.dma('sp', y_d.ap()[q0:q0 + 64, ycol0 + hh * 64: ycol0 + (hh + 1) * 64], ys[i2][:, :])


def na_host_tables(rpb_l, hsel):
    kc = np.arange(64)[:, None]
    qc = np.arange(64)[None, :]
    idx = np.clip(kc - qc + 15, 0, 30)
    g = rpb_l[hsel][:, :, idx]
    g = np.ascontiguousarray(np.transpose(g, (2, 0, 1, 3))).astype(np.float32)
    c0 = np.clip(qc - 8, 0, 48)
    inw = (kc >= c0) & (kc < c0 + 16)
    mask = np.where(inw, 0.0, -1e30).astype(np.float32)
    return g, mask


T = 2304
TC = 256
SEGS = [(0, 256), (256, 2304)]
TBLK = [(i * 512, min(512, T - i * 512)) for i in range(5)]
WDS = 0.606531
TCH = 16
ORDER = 4
R32 = True
F32R = mybir.dt.float32r
KEEP = 3
NARR = 26


def rwkv(kb, zT_d, row0, prm, rws_d, yT_d, yrow0, tag, consts):
    bones, ident2 = consts['bones'], consts['ident2']
    R = rws_d.ap()
    kb.push()
    P = {}
    for k_, shp in (('mu', [128, 9, 2]), ('w0', [128, 2, 2]), ('a0', [128, 2, 2]), ('kk', [128, 2]), ('ka', [128, 2]),
                    ('w2', [128, 256]), ('a2', [128, 256]), ('g2', [128, 256])):
        P[k_] = kb.sb(tag + "_p_" + k_, shp)
        kb.dma('sp', P[k_][tuple(slice(None) for _ in shp)], prm[k_].ap())
    c0 = kb.sb(tag + "_c0", [128, 9])
    kb.tt(c0[:, :], P['mu'][:, :, 0], P['mu'][:, :, 1], ALU.add)
    kb.ts(c0[:, :], c0[:, :], -1.0, ALU.mult, 1.0, ALU.add)
    omka = kb.sb(tag + "_omka", [128, 2])
    kb.ts(omka[:, :], P['ka'][:, :], -1.0, ALU.mult, 1.0, ALU.add)
    zs = kb.sb(tag + "_zs", [128, 9, T])
    zin = [kb.sb(tag + "_zin%d" % i, [128, T]) for i in range(2)]
    for c in range(9):
        z = zin[c % 2]
        kb.dma('sp', z[:, :], zT_d.ap()[row0 + c * 128: row0 + (c + 1) * 128, :])
        kb.ts(zs[:, c, :], z[:, :], c0[:, c:c + 1], ALU.mult, e='pool' if c % 2 else 'dve')
        for (s0, s1) in SEGS:
            kb.stt(zs[:, c, s0 + 1:s1], z[:, s0:s1 - 1], P['mu'][:, c, 0:1], zs[:, c, s0 + 1:s1], ALU.mult, ALU.add)
            kb.stt(zs[:, c, s0:s1 - 1], z[:, s0 + 1:s1], P['mu'][:, c, 1:2], zs[:, c, s0:s1 - 1], ALU.mult, ALU.add)
    kb.act(zs[:, 6, :], zs[:, 6, :], AF.Tanh)
    kb.act(zs[:, 8, :], zs[:, 8, :], AF.Sigmoid)
    pp = [kb.ps(tag + "_pp%d" % i, [128, 512]) for i in range(4)]
    pi = [0]

    def nps():
        pi[0] += 1
        return pp[pi[0] % 4]
    ta = kb.sb(tag + "_ta", [128, T])
    tb_ = kb.sb(tag + "_tb", [128, T])
    a_sb = [kb.sb(tag + "_a%d" % d, [128, T]) for d in range(2)]
    kk_sb = kb.sb(tag + "_kk", [128, T])
    for c2 in range(2):
        cs = slice(c2 * 128, (c2 + 1) * 128)
        kb.dma('sp', R[c2 * 3 + 1], zs[:, 0 + c2, :])
        kb.dma('sp', R[c2 * 3 + 2], zs[:, 4 + c2, :])
        kb.dma('sp', R[20 + c2], zs[:, 2 + c2, :])
        for d in range(2):
            ds = slice(64 * d, 64 * d + 64)
            for (t0, n) in TBLK:
                p = nps()
                kb.mm(p[:, :n], P['w2'][ds, cs], zs[ds, 6, t0:t0 + n])
                kb.act(ta[:, t0:t0 + n], p[:, :n], AF.Sigmoid, bias=P['w0'][:, c2, d:d + 1])
                p = nps()
                kb.mm(p[:, :n], P['a2'][ds, cs], zs[ds, 7, t0:t0 + n])
                kb.act(a_sb[d][:, t0:t0 + n], p[:, :n], AF.Sigmoid, bias=P['a0'][:, c2, d:d + 1])
            kb.act(ta[:, :], ta[:, :], AF.Exp, scale=-WDS)
            kb.dma('sp', R[6 + (c2 * 2 + d) * 3 + 0], ta[:, :])
        for (t0, n) in TBLK:
            p = nps()
            kb.mm(p[:, :n], P['g2'][:, cs], zs[:, 8, t0:t0 + n])
            kb.copy(ta[:, t0:t0 + n], p[:, :n], e='act')
        kb.dma('sp', R[18 + c2], ta[:, :])
        kb.ts(kk_sb[:, :], zs[:, 2 + c2, :], P['kk'][:, c2:c2 + 1], ALU.mult)
        kb.tt(tb_[:, :], kk_sb[:, :], kk_sb[:, :], ALU.mult, e='pool')
        for (t0, n) in TBLK:
            p = nps()
            kb.mm(p[:, :n], bones[:, :], tb_[:, t0:t0 + n])
            kb.act(ta[:, t0:t0 + n], p[:, :n], AF.Sqrt)
        kb.ts(ta[:, :], ta[:, :], 1e-12, ALU.max)
        kb.recip(ta[:, :], ta[:, :])
        kb.tt(kk_sb[:, :], kk_sb[:, :], ta[:, :], ALU.mult)
        kb.ts(tb_[:, :], kk_sb[:, :], -1.0, ALU.mult)
        kb.dma('sp', R[c2 * 3 + 0], tb_[:, :])
        for d in range(2):
            kb.tt(ta[:, :], kk_sb[:, :], a_sb[d][:, :], ALU.mult)
            kb.dma('sp', R[6 + (c2 * 2 + d) * 3 + 1], ta[:, :])
            kb.ts(tb_[:, :], a_sb[d][:, :], P['ka'][:, c2:c2 + 1], ALU.mult, omka[:, c2:c2 + 1], ALU.add)
            kb.tt(tb_[:, :], tb_[:, :], zs[:, 2 + c2, :], ALU.mult)
            kb.dma('sp', R[6 + (c2 * 2 + d) * 3 + 2], tb_[:, :])
    kb.pop()
    kb.push()
    rwkv_scan(kb, rws_d, consts, tag)
    kb.pop()
    kb.push()
    Q = {}
    for k_ in ('rk', 'lnw', 'lnb'):
        Q[k_] = kb.sb(tag + "_q_" + k_, [128, 2])
        kb.dma('sp', Q[k_][:, :], prm[k_].ap())
    pp = [kb.ps(tag + "_op%d" % i, [128, 512]) for i in range(4)]
    y = kb.sb(tag + "_y", [128, T])
    y2 = kb.sb(tag + "_y2", [128, T])
    t1 = kb.sb(tag + "_t1", [128, T])
    t2 = kb.sb(tag + "_t2", [128, T])
    t3 = kb.sb(tag + "_t3", [128, T])
    for c2 in range(2):
        kb.dma('sp', y[:, :], R[22 + 0 * 2 + c2])
        kb.dma('act', y2[:, :], R[22 + 1 * 2 + c2])
        kb.tt(y[:, :], y[:, :], y2[:, :], ALU.add)
        for (t0, n) in TBLK:
            p = nps()
            kb.mm(p[:, :n], bones[:, :], y[:, t0:t0 + n])
            kb.stt(t1[:, t0:t0 + n], p[:, :n], -1.0 / 64, y[:, t0:t0 + n], ALU.mult, ALU.add)
        kb.tt(t2[:, :], t1[:, :], t1[:, :], ALU.mult, e='pool')
        for (t0, n) in TBLK:
            p = nps()
            kb.mm(p[:, :n], bones[:, :], t2[:, t0:t0 + n])
            kb.ts(t3[:, t0:t0 + n], p[:, :n], 1.0 / 64, ALU.mult, 64e-5, ALU.add)
        kb.act(t3[:, :], t3[:, :], AF.Sqrt)
        kb.recip(t3[:, :], t3[:, :])
        kb.tt(t1[:, :], t1[:, :], t3[:, :], ALU.mult)
        kb.ts(t1[:, :], t1[:, :], Q['lnw'][:, c2:c2 + 1], ALU.mult, Q['lnb'][:, c2:c2 + 1], ALU.add)
        kb.dma('sp', y[:, :], R[c2 * 3 + 1])
        kb.dma('act', y2[:, :], R[20 + c2])
        kb.stt(t2[:, :], y[:, :], Q['rk'][:, c2:c2 + 1], y2[:, :], ALU.mult, ALU.mult)
        kb.dma('sp', y2[:, :], R[c2 * 3 + 2])
        for (t0, n) in TBLK:
            p = nps()
            kb.mm(p[:, :n], bones[:, :], t2[:, t0:t0 + n])
            kb.tt(t3[:, t0:t0 + n], p[:, :n], y2[:, t0:t0 + n], ALU.mult)
        kb.tt(t1[:, :], t1[:, :], t3[:, :], ALU.add)
        kb.dma('sp', y[:, :], R[18 + c2])
        kb.tt(t1[:, :], t1[:, :], y[:, :], ALU.mult)
        kb.dma('sp', yT_d.ap()[yrow0 + c2 * 128: yrow0 + (c2 + 1) * 128, :], t1[:, :])
    kb.pop()


def rwkv_scan(kb, rws_d, consts, tag):
    bones, ident2 = consts['bones'], consts['ident2']
    R = rws_d.ap()
    nchunk = T // TCH
    nctx = TC // TCH
    G = [(d, c2) for d in range(2) for c2 in range(2)]
    S = [[kb.sb(tag + "_S%d_%d" % (g, i), [128, 64]) for i in range(2)] for g in range(4)]
    zt_ = kb.sb(tag + "_zt", [128, 64])
    kb.memset(zt_[:, :], 0.0)
    for g in range(4):
        if R32:
            kb.copy(S[g][0][:, :].bitcast(F32R), zt_[:, :])
        else:
            kb.memset(S[g][0][:, :], 0.0)
    sh = [[kb.sb(tag + "_sh%d_%d" % (g, i), [128, 3, TCH]) for i in range(2)] for g in range(4)]
    pd = [[kb.sb(tag + "_pd%d_%d" % (g, i), [128, 3, TCH]) for i in range(2)] for g in range(4)]
    Vd = [kb.sb(tag + "_Vd%d" % i, [128, TCH, 64]) for i in range(2)]
    KV = [[kb.sb(tag + "_KV%d_%d" % (g, i), [128, TCH, 64]) for i in range(2)] for g in range(4)]
    Abc = [[kb.sb(tag + "_Ab%d_%d" % (g, i), [128, TCH, 128 if R32 else 64]) for i in range(2)] for g in range(4)]
    tmp = [kb.sb(tag + "_tmp%d" % g, [128, 64]) for g in range(4)]
    Rm = [[kb.sb(tag + "_Rm%d_%d" % (g, i), [128, TCH, 2]) for i in range(2)] for g in range(4)]
    ysb = [kb.sb(tag + "_ysb%d" % i, [64, 4, 2, TCH]) for i in range(2)]
    pvb = [kb.ps(tag + "_pvb%d" % i, [128, TCH * 64]) for i in range(1)]
    psa = [kb.ps(tag + "_psa%d" % i, [128, 512]) for i in range(4)]
    pys = [kb.ps(tag + "_pys%d" % i, [128, 4, 128]) for i in range(2)]
    par = [0, 0, 0, 0]
    vi = 0
    pend = []
    late = []
    for n in range(nchunk):
        nb = n % 2
        t0s = []
        for g, (d, c2) in enumerate(G):
            if d == 0:
                t0 = n * TCH
            else:
                t0 = (nctx - 1 - n) * TCH if n < nctx else (nchunk - 1 - (n - nctx)) * TCH
            t0s.append(t0)
            kb.dma('sp', sh[g][nb][:, :, :], R[c2 * 3: c2 * 3 + 3, :, t0:t0 + TCH].rearrange("a p t -> p a t"))
            b0 = 6 + (c2 * 2 + d) * 3
            kb.dma('act', pd[g][nb][:, :, :], R[b0: b0 + 3, :, t0:t0 + TCH].rearrange("a p t -> p a t"))
            vd = Vd[vi % 2]
            pv = pvb[0]
            vi += 1
            kb.tt(vd[:, :, :], ident2[:, :].unsqueeze(1).to_broadcast([128, TCH, 64]),
                  sh[g][nb][:, 2, :].unsqueeze(2).to_broadcast([128, TCH, 64]), ALU.mult, e='pool')
            vflat = vd[:, :, :].rearrange("p t i -> p (t i)")
            for j in range(TCH * 64 // 512):
                kb.mm(pv[:, j * 512:(j + 1) * 512], bones[:, :], vflat[:, j * 512:(j + 1) * 512])
            kb.tt(KV[g][nb][:, :, :], pv[:, :].rearrange("p (t i) -> p t i", i=64),
                  pd[g][nb][:, 2, :].unsqueeze(2).to_broadcast([128, TCH, 64]), ALU.mult)
            for h_ in range(2):
                kb.ts(Rm[g][nb][:, :, h_], sh[g][nb][:, 1, :], bones[:, 64 * h_:64 * h_ + 1], ALU.mult, e='pool')
            if R32:
                kb.tt(Abc[g][nb][:, :, :].bitcast(F32R), sh[g][nb][:, 0, :].unsqueeze(2).to_broadcast([128, TCH, 128]),
                      bones[:, :].unsqueeze(1).to_broadcast([128, TCH, 128]), ALU.mult, e='pool')
            else:
                kb.copy(Abc[g][nb][:, :, :], sh[g][nb][:, 0, :].unsqueeze(2).to_broadcast([128, TCH, 64]), e='pool')
        yp = pys[nb]

        def emit_sa(g, s):
            d, c2 = G[g]
            si = s if d == 0 else TCH - 1 - s
            Sc = S[g][par[g]]
            Sn = S[g][1 - par[g]]
            par[g] = 1 - par[g]
            if R32:
                kb.mm(psa[g][:, 0:64], Abc[g][nb][:, si, :].bitcast(F32R), Sc[:, :].bitcast(F32R), r=[Abc[g][nb], Sc], w=[psa[g]])
            else:
                for h in range(2):
                    hs = slice(64 * h, 64 * h + 64)
                    kb.mm(psa[g][hs, 0:64], Abc[g][nb][hs, si, :], Sc[hs, :], r=[Abc[g][nb], Sc], w=[psa[g]])
            return (g, Sc, Sn, si)

        def flush_one():
            if not pend:
                return
            (g_, Sn_, si_, yp_, rm_, nb_) = pend.pop(0)
            kb.mm(yp_[0:64, g_, 2 * si_:2 * si_ + 2], Sn_[:, :], rm_[:, si_, :], r=[Sn_, rm_], w=[('y', nb_, g_)])
        for s in range(TCH):
            items = []
            for g in range(4):
                it = emit_sa(g, s)
                flush_one()
                (g_, Sc, Sn, si) = it
                kb.stt(tmp[g][:, :], Sc[:, :], pd[g][nb][:, 0, si:si + 1], KV[g][nb][:, si, :], ALU.mult, ALU.add)
                items.append(it)
            if s == 0 and late:
                (pnb, pt0s) = late.pop()
                kb.copy(ysb[pnb][:, :, :, :], pys[pnb][0:64, :, 0:2 * TCH].rearrange("p g (t h) -> p g h t", h=2), e='act', r=[('y', pnb, g) for g in range(4)], w=[ysb[pnb]])
                for g, (d, c2) in enumerate(G):
                    for h_ in range(2):
                        kb.dma('sp', R[22 + d * 2 + c2, 64 * h_:64 * h_ + 64, pt0s[g]:pt0s[g] + TCH], ysb[pnb][:, g, h_, :])
            for (g, Sc, Sn, si) in items:
                kb.stt(Sn[:, :].bitcast(F32R) if R32 else Sn[:, :], psa[g][:, 0:64], pd[g][nb][:, 1, si:si + 1], tmp[g][:, :], ALU.mult, ALU.add)
                pend.append((g, Sn, si, yp, Rm[g][nb], nb))
        late.append((nb, list(t0s)))
    while pend:
        flush_one()
    (pnb, pt0s) = late.pop()
    kb.copy(ysb[pnb][:, :, :, :], pys[pnb][0:64, :, 0:2 * TCH].rearrange("p g (t h) -> p g h t", h=2), e='act', r=[('y', pnb, g) for g in range(4)], w=[ysb[pnb]])
    for g, (d, c2) in enumerate(G):
        for h_ in range(2):
            kb.dma('sp', R[22 + d * 2 + c2, 64 * h_:64 * h_ + 64, pt0s[g]:pt0s[g] + TCH], ysb[pnb][:, g, h_, :])
    return
    if False:
        nb = 0
        t0s = [0] * 4
        kb.copy(ysb[nb][:, :, :], yp[:, :, 0:TCH], e='act', r=[('y', nb, g) for g in range(4)], w=[ysb[nb]])
        for g, (d, c2) in enumerate(G):
            kb.dma('sp', R[22 + d * 2 + c2, :, t0s[g]:t0s[g] + TCH], ysb[nb][:, g, :])


T = 2304
TC = 256
NT = 18
SEGS = [(0, 256), (256, 2304)]
TBLK = [(i * 512, min(512, T - i * 512)) for i in range(5)]
EPS = 1e-6


def mamba(kb, zT_d, row0, ztok_d, col0, prm, cst, y_d, ycol0, tag):
    ident = cst['ident']
    xs_tok = kb.sb(tag + "_xstok", [128, NT, 512])
    B_tok = kb.sb(tag + "_Btok", [128, NT, 256])
    BT = [kb.sb(tag + "_BT%d" % g, [128, T]) for g in range(2)]
    CT = [kb.sb(tag + "_CT%d" % g, [128, T]) for g in range(2)]
    kb.push()
    cw = kb.sb(tag + "_cw", [128, 8, 5])
    cb = kb.sb(tag + "_cb", [128, 8])
    kb.dma('sp', cw[:, :, :], prm['cw'].ap())
    kb.dma('sp', cb[:, :], prm['cb'].ap())
    pmT = kb.sb(tag + "_pmT", [128, 128])
    cos = kb.sb(tag + "_cos", [128, 2048])
    sin = kb.sb(tag + "_sin", [128, 2048])
    kb.dma('sp', pmT[:, :], cst['pmT'].ap())
    kb.dma('sp', cos[:, :], cst['cos'].ap())
    kb.dma('act', sin[:, :], cst['sin'].ap())
    zin = [kb.sb(tag + "_zin%d" % i, [128, T]) for i in range(2)]
    cv = [kb.sb(tag + "_cv%d" % i, [128, T]) for i in range(2)]
    tr = kb.sb(tag + "_tr", [128, 2048])
    pp = [kb.ps(tag + "_pp%d" % i, [128, 512]) for i in range(4)]
    pi = [0]

    def nps():
        pi[0] += 1
        return pp[pi[0] % 4]
    for c in range(8):
        z = zin[c % 2]
        kb.dma('sp', z[:, :], zT_d.ap()[row0 + c * 128: row0 + (c + 1) * 128, :])
        if c < 4:
            o = cv[c % 2]
        elif c < 6:
            o = BT[c - 4]
        else:
            o = CT[c - 6]
        kb.ts(o[:, :], z[:, :], cw[:, c, 2:3], ALU.mult, cb[:, c:c + 1], ALU.add)
        for (s0, s1) in SEGS:
            for i in (0, 1, 3, 4):
                of = i - 2
                a0, a1 = s0 + max(0, -of), s1 - max(0, of)
                kb.stt(o[:, a0:a1], z[:, a0 + of:a1 + of], cw[:, c, i:i + 1], o[:, a0:a1], ALU.mult, ALU.add)
        kb.act(o[:, :], o[:, :], AF.Silu)
        if c >= 4:
            for j in range(4):
                p = nps()
                kb.mm(p[:, :], pmT[:, :], o[:, TC + j * 512: TC + (j + 1) * 512])
                kb.tt(tr[:, j * 512:(j + 1) * 512], p[:, :], sin[:, j * 512:(j + 1) * 512], ALU.mult)
            kb.tt(o[:, TC:T], o[:, TC:T], cos[:, :], ALU.mult, e='pool')
            kb.tt(o[:, TC:T], o[:, TC:T], tr[:, :], ALU.add)
        if c < 6:
            for ti in range(NT):
                p = nps()
                kb.transpose(p[:, 0:128], o[:, ti * 128:(ti + 1) * 128], ident[:, :])
                dst = xs_tok[:, ti, c * 128:(c + 1) * 128] if c < 4 else B_tok[:, ti, (c - 4) * 128:(c - 3) * 128]
                kb.copy(dst, p[:, 0:128], e='act' if ti % 2 else 'dve')
    kb.pop()
    kb.push()
    y_acc = kb.sb(tag + "_yacc", [128, NT, 512])
    dtr = kb.sb(tag + "_dtr", [128, NT, 16])
    kb.dma('sp', dtr[:, :, :], ztok_d.ap()[:, col0 + 512: col0 + 528].rearrange("(n p) c -> p n c", p=128))
    dtb = kb.sb(tag + "_dtb", [128, 16])
    aneg = kb.sb(tag + "_aneg", [128, 16])
    kb.dma('sp', dtb[:, :], prm['dtb'].ap().partition_broadcast(128))
    kb.dma('sp', aneg[:, :], prm['alog'].ap().partition_broadcast(128))
    kb.act(aneg[:, :], aneg[:, :], AF.Exp)
    kb.ts(aneg[:, :], aneg[:, :], -1.0, ALU.mult)
    dt = kb.sb(tag + "_dt", [128, NT, 16])
    dA = kb.sb(tag + "_dA", [128, NT, 16])
    kb.tt(dt[:, :, :], dtr[:, :, :], dtb[:, :].unsqueeze(1).to_broadcast([128, NT, 16]), ALU.add)
    kb.act(dt[:, :, :], dt[:, :, :], AF.Exp)
    kb.ts(dt[:, :, :], dt[:, :, :], 1.0, ALU.add)
    kb.act(dt[:, :, :], dt[:, :, :], AF.Ln)
    kb.tt(dA[:, :, :], dt[:, :, :], aneg[:, :].unsqueeze(1).to_broadcast([128, NT, 16]), ALU.mult)
    tri = kb.sb(tag + "_tri", [128, 2, 128])
    nmask = kb.sb(tag + "_nmask", [128, 2, 128])
    kb.dma('sp', tri[:, :, :], cst['tri'].ap().rearrange("d p l -> p d l"))
    kb.dma('sp', nmask[:, :, :], cst['nmask'].ap().rearrange("d p l -> p d l"))
    Hs = [kb.sb(tag + "_H%d" % d, [128, 8, 64]) for d in range(2)]
    for d in range(2):
        kb.memset(Hs[d][:, :, :], 0.0)
    p_acs = kb.ps(tag + "_pacs", [128, 8, 128])
    p_col = kb.ps(tag + "_pcol", [128, 8])
    p_G = kb.ps(tag + "_pG", [128, 2, 128])
    p_y = kb.ps(tag + "_py", [128, 8, 64])
    p_yo = kb.ps(tag + "_pyo", [128, 8, 64])
    p_H = kb.ps(tag + "_pH", [128, 8, 64])
    dAb = kb.sb(tag + "_dAb", [128, 8, 128])
    acol = kb.sb(tag + "_acol", [128, 8])
    DT_ = kb.sb(tag + "_DT", [128, 8, 128])
    WT = kb.sb(tag + "_WT", [128, 8, 128])
    xdt = kb.sb(tag + "_xdt", [128, 8, 64])
    xdts = kb.sb(tag + "_xdts", [128, 8, 64])
    eac = kb.sb(tag + "_eac", [128, 8])
    decs = kb.sb(tag + "_decs", [128, 8])
    etot = kb.sb(tag + "_etot", [128, 8])
    ytmp = kb.sb(tag + "_ytmp", [128, 8, 64])
    order = [list(range(NT)), [1, 0] + list(range(NT - 1, 1, -1))]
    seen = set()
    for n in range(NT):
        for d in range(2):
            ti = order[d][n]
            tsl = slice(ti * 128, (ti + 1) * 128)
            hsl = slice(d * 8, d * 8 + 8)
            llast = 127 if d == 0 else 0
            kb.mm(p_col[:, :], tri[:, d, :], dA[:, ti, hsl])
            kb.copy(acol[:, :], p_col[:, :], e='act')
            kb.copy(dAb[:, :, :], dA[:, ti, hsl].unsqueeze(2).to_broadcast([128, 8, 128]), e='pool')
            for h in range(8):
                kb.mm(p_acs[:, h, :], dAb[:, h, :], tri[:, d, :])
            kb.tt(DT_[:, :, :], p_acs[:, :, :], nmask[:, d, :].unsqueeze(1).to_broadcast([128, 8, 128]), ALU.add)
            kb.tt(DT_[:, :, :], DT_[:, :, :], acol[:, :].unsqueeze(2).to_broadcast([128, 8, 128]), ALU.subtract)
            kb.act(DT_[:, :, :], DT_[:, :, :], AF.Exp)
            for g in range(2):
                kb.mm(p_G[:, g, :], BT[g][:, tsl], CT[g][:, tsl])
            for g in range(2):
                kb.tt(WT[:, g * 4:(g + 1) * 4, :], DT_[:, g * 4:(g + 1) * 4, :],
                      p_G[:, g, :].unsqueeze(1).to_broadcast([128, 4, 128]), ALU.mult)
            kb.tt(xdt[:, :, :], xs_tok[:, ti, :].rearrange("p (h q) -> p h q", q=64),
                  dt[:, ti, hsl].unsqueeze(2).to_broadcast([128, 8, 64]), ALU.mult, e='pool')
            for h in range(8):
                kb.mm(p_y[:, h, :], WT[:, h, :], xdt[:, h, :])
            for g in range(2):
                kb.mm(p_yo[:, g * 4:(g + 1) * 4, :], CT[g][:, tsl], Hs[d][:, g * 4:(g + 1) * 4, :])
            kb.act(eac[:, :], acol[:, :], AF.Exp)
            kb.tt(ytmp[:, :, :], p_yo[:, :, :], eac[:, :].unsqueeze(2).to_broadcast([128, 8, 64]), ALU.mult)
            ya = y_acc[:, ti, :].rearrange("p (h q) -> p h q", q=64)
            if ti in seen:
                kb.tt(ytmp[:, :, :], ytmp[:, :, :], p_y[:, :, :], ALU.add)
                kb.tt(ya, ya, ytmp[:, :, :], ALU.add, e='pool')
            else:
                kb.tt(ya, ytmp[:, :, :], p_y[:, :, :], ALU.add)
                seen.add(ti)
            kb.tt(decs[:, :], p_acs[:, :, llast], acol[:, :], ALU.subtract)
            kb.act(decs[:, :], decs[:, :], AF.Exp)
            kb.act(etot[:, :], p_acs[:, :, llast], AF.Exp)
            kb.tt(xdts[:, :, :], xdt[:, :, :], decs[:, :].unsqueeze(2).to_broadcast([128, 8, 64]), ALU.mult)
            for g in range(2):
                kb.mm(p_H[:, g * 4:(g + 1) * 4, :], B_tok[:, ti, g * 128:(g + 1) * 128], xdts[:, g * 4:(g + 1) * 4, :])
            kb.tt(Hs[d][:, :, :], Hs[d][:, :, :], etot[:, :].unsqueeze(2).to_broadcast([128, 8, 64]), ALU.mult)
            kb.tt(Hs[d][:, :, :], Hs[d][:, :, :], p_H[:, :, :], ALU.add)
    dsk = kb.sb(tag + "_dsk", [128, 8])
    nw = kb.sb(tag + "_nw", [128, 512])
    kb.dma('sp', dsk[:, :], prm['dsk'].ap().partition_broadcast(128))
    kb.dma('sp', nw[:, :], prm['nw'].ap().partition_broadcast(128))
    gt = [kb.sb(tag + "_gt%d" % i, [128, 512]) for i in range(2)]
    yo = [kb.sb(tag + "_yo%d" % i, [128, 512]) for i in range(2)]
    sq = kb.sb(tag + "_sq", [128, 512])
    ss = kb.sb(tag + "_ss", [128, 2])
    for ti in range(NT):
        g_ = gt[ti % 2]
        y_ = yo[ti % 2]
        kb.dma('act', g_[:, :], ztok_d.ap()[ti * 128:(ti + 1) * 128, col0:col0 + 512])
        kb.act(g_[:, :], g_[:, :], AF.Silu)
        kb.tt(y_[:, :].rearrange("p (h q) -> p h q", q=64), xs_tok[:, ti, :].rearrange("p (h q) -> p h q", q=64),
              dsk[:, :].unsqueeze(2).to_broadcast([128, 8, 64]), ALU.mult, e='pool')
        kb.tt(y_[:, :], y_[:, :], y_acc[:, ti, :], ALU.add)
        kb.tt(y_[:, :], y_[:, :], g_[:, :], ALU.mult)
        kb.tt(sq[:, :], y_[:, :], y_[:, :], ALU.mult, e='pool')
        kb.op('dve', lambda eng, o=ss, i=sq: eng.reduce_sum(o[:, :], i[:, :].rearrange("p (g q) -> p g q", q=256), axis=mybir.AxisListType.X),
              r=[sq], w=[ss])
        kb.ts(ss[:, :], ss[:, :], 1.0 / 256, ALU.mult, EPS, ALU.add)
        kb.act(ss[:, :], ss[:, :], AF.Sqrt)
        kb.recip(ss[:, :], ss[:, :])
        kb.tt(y_[:, :].rearrange("p (g q) -> p g q", q=256), y_[:, :].rearrange("p (g q) -> p g q", q=256),
              ss[:, :].unsqueeze(2).to_broadcast([128, 2, 256]), ALU.mult)
        kb.tt(y_[:, :], y_[:, :], nw[:, :], ALU.mult)
        kb.dma('sp', y_d.ap()[ti * 128:(ti + 1) * 128, ycol0:ycol0 + 512], y_[:, :])
    kb.pop()


def mb_consts():
    nf = 32
    inv = (10000.0 ** (-np.arange(nf, dtype=np.float32) / nf)).astype(np.float32)
    pos = np.arange(2048)
    row = (pos // 64).astype(np.float32)
    col = (pos % 64).astype(np.float32)
    cos = np.zeros((128, 2048), np.float32)
    sin = np.zeros((128, 2048), np.float32)
    for n in range(128):
        p = row if n < 64 else col
        ang = (p * inv[n % 32]).astype(np.float32)
        cos[n] = np.cos(ang)
        sin[n] = np.sin(ang)
    pm = np.zeros((128, 128), np.float32)
    for n in range(128):
        if (n % 64) < 32:
            pm[n, n + 32] = -1.0
        else:
            pm[n, n - 32] = 1.0
    s = np.arange(128)[:, None]
    l = np.arange(128)[None, :]
    tri = np.stack([(s <= l), (s >= l)]).astype(np.float32)
    nmask = np.where(tri > 0, 0.0, -1e30).astype(np.float32)
    return dict(pmT=np.ascontiguousarray(pm.T), cos=cos, sin=sin, tri=tri, nmask=nmask)


NTOK = 1152
NTI = 9
D = 2048
DCH = 16
EPS = 1e-6
P2BLK = [(0, 128, 0), (128, 512, 1), (640, 512, 1)]
KG = 2
NGRP = 128 // KG
TP = 384
NPASS = NTOK // TP


def p2(kb, yT_d, xT_d, prm, cst, xo_d, x1T_d, h2T_d, tag, final_nw=None, ntok=NTOK, halves=None, ctx_tiles=1):
    ones, ident = cst['ones'], cst['ident']
    if halves is None:
        halves = [(0, NTOK, P2BLK)]
    npass = ntok // TP
    xv = xT_d.ap().rearrange("(c p) t -> p c t", p=128)
    yv = yT_d.ap().rearrange("(c p) t -> p c t", p=128)
    x1v = x1T_d.ap().rearrange("(c p) t -> p c t", p=128)
    h2v = h2T_d.ap().rearrange("(c p) t -> p c t", p=128)
    xov = xo_d.ap().rearrange("(c p) t -> p c t", p=128)
    M = {}
    for k_ in ('g1', 'sh2', 'sc2', 'g2'):
        M[k_] = kb.sb(tag + "_m_" + k_, [128, DCH, 2])
        kb.dma('sp', M[k_][:, :, :], prm[k_].ap())
    nw2 = kb.sb(tag + "_nw2", [128, DCH])
    kb.dma('sp', nw2[:, :], prm['nw2'].ap())
    kb.push()
    xT = kb.sb(tag + "_xT", [128, DCH, NTOK])
    yT = kb.sb(tag + "_yT", [128, DCH, NTOK])
    wt = [kb.sb(tag + "_wo%d" % i, [128, DCH, 128]) for i in range(2)]
    pz = [kb.ps(tag + "_pz%d" % i, [128, 512]) for i in range(4)]
    wv = prm['wout'].ap().rearrange("(c p) n -> p c n", p=128)
    A = kb.sb(tag + "_A", [128, DCH, 2])
    kb.ts(A[:, :, :], M['sc2'][:, :, :], 1.0, ALU.add)
    kb.tt(A[:, :, :], A[:, :, :], nw2[:, :].unsqueeze(2).to_broadcast([128, DCH, 2]), ALU.mult)
    sq = kb.sb(tag + "_sq", [128, 512])
    rstd = kb.sb(tag + "_rstd", [128, 512])
    pi = 0
    wi = 0
    for (off, nh, blks) in halves:
        for c in range(DCH):
            kb.dma('sp' if c % 2 else 'act', xT[:, c, :nh], xv[:, c, off:off + nh])
            kb.dma('act' if c % 2 else 'sp', yT[:, c, :nh], yv[:, c, off:off + nh])
        for dc in range(DCH):
            w = wt[wi % 2]
            wi += 1
            kb.dma('sp', w[:, :, :], wv[:, :, dc * 128:(dc + 1) * 128])
            for (t0, n, seg) in blks:
                p = pz[pi % 4]
                pi += 1
                for mc in range(DCH):
                    kb.mm(p[:, :n], w[:, mc, :], yT[:, mc, t0:t0 + n], start=(mc == 0), stop=(mc == DCH - 1))
                kb.stt(xT[:, dc, t0:t0 + n], p[:, :n], M['g1'][:, dc, seg:seg + 1], xT[:, dc, t0:t0 + n], ALU.mult, ALU.add)
        kb.dma('sp', x1v[:, :, off:off + nh], xT[:, :, :nh])
        for (t0, n, seg) in blks:
            p = pz[pi % 4]
            pi += 1
            for c in range(DCH):
                s_ = sq if c % 2 == 0 else rstd
                kb.act(s_[:, :n], xT[:, c, t0:t0 + n], AF.Square)
                kb.mm(p[:, :n], ones[:, :], s_[:, :n], start=(c == 0), stop=(c == DCH - 1))
            kb.ts(rstd[:, :n], p[:, :n], 1.0 / D, ALU.mult, EPS, ALU.add)
            kb.act(rstd[:, :n], rstd[:, :n], AF.Sqrt)
            kb.recip(rstd[:, :n], rstd[:, :n])
            for c in range(DCH):
                kb.tt(yT[:, c, t0:t0 + n], xT[:, c, t0:t0 + n], rstd[:, :n], ALU.mult)
                kb.act(yT[:, c, t0:t0 + n], yT[:, c, t0:t0 + n], AF.Identity, bias=M['sh2'][:, c, seg:seg + 1], scale=A[:, c, seg:seg + 1])
        kb.dma('sp', h2v[:, :, off:off + nh], yT[:, :, :nh])
    kb.pop()
    kb.push()
    keysT = kb.sb(tag + "_keysT", [128, 16, 128])
    kb.dma('sp', keysT[:, :, :], prm['keysT'].ap())
    h2 = kb.sb(tag + "_h2", [128, DCH, TP])
    S = kb.sb(tag + "_S", [128, 3, 16, 128])
    qT = [kb.sb(tag + "_qT%d" % i, [128, TP]) for i in range(2)]
    wq = [kb.sb(tag + "_wq%d" % i, [128, DCH, 128]) for i in range(2)]
    wqv = prm['wq'].ap().rearrange("(c p) n -> p c n", p=128)
    thr = kb.sb(tag + "_thr", [128, 3, 8])
    ncc = kb.sb(tag + "_ncc", [128, 3, 8])
    t0a = kb.sb(tag + "_t0a", [128, 16])
    t1a = kb.sb(tag + "_t1a", [128, 16])
    tmp128 = kb.sb(tag + "_tmp128", [128, 128])
    cand = kb.sb(tag + "_cand", [128, 16, 16])
    cand2 = kb.sb(tag + "_cand2", [128, 256])
    best = kb.sb(tag + "_best", [128, 24])
    eb = kb.sb(tag + "_eb", [128, 16])
    zs = kb.sb(tag + "_zsum", [128, 1])
    nmx = kb.sb(tag + "_nmx", [128, 1])
    BF = mybir.dt.bfloat16
    UTt = [kb.sb(tag + "_UT%d" % i, [128, DCH, KG * 128], BF) for i in range(3)]
    Vt = [kb.sb(tag + "_V%d" % i, [128, KG, D], BF) for i in range(3)]
    h2b = kb.sb(tag + "_h2b", [128, DCH, TP], BF)
    UTv = prm['UT'].ap().rearrange("(c p) e -> p c e", p=128)
    Vv = prm['V'].ap().rearrange("(k q) d -> q k d", q=128)
    G = [[kb.sb(tag + "_G%d_%d" % (b_, i), [128, KG * 128]) for i in range(3)] for b_ in range(3)]
    tgs = [kb.sb(tag + "_tg%d" % i, [128, KG, 128]) for i in range(4)]
    egs = [kb.sb(tag + "_eg%d" % i, [128, KG, 128]) for i in range(4)]
    gti = 0
    gA = [kb.sb(tag + "_gA%d" % i, [128, TP]) for i in range(2)]
    GAT = [kb.sb(tag + "_GAT%d" % i, [128, KG, TP], BF) for i in range(2)]
    facc = kb.sb(tag + "_facc", [128, 3, D])
    xr = h2
    pA = [kb.ps(tag + "_pA%d" % i, [128, 512]) for i in range(2)]
    pT = [kb.ps(tag + "_pT%d" % i, [128, 512]) for i in range(2)]
    pF = [kb.ps(tag + "_pF%d" % i, [128, 512]) for i in range(2)]
    pS = [kb.ps(tag + "_pS%d" % i, [128, 512]) for i in range(2)]
    ia = it_ = if_ = is_ = 0
    gi = 0
    for tp in range(npass):
        tk0 = tp * TP
        kb.dma('sp', h2[:, :, :], h2v[:, :, tk0:tk0 + TP])
        for hp in range(16):
            w = wq[hp % 2]
            kb.dma('act', w[:, :, :], wqv[:, :, hp * 128:(hp + 1) * 128])
            p = pA[ia % 2]
            ia += 1
            for c in range(DCH):
                kb.mm(p[:, :TP], w[:, c, :], h2[:, c, :], start=(c == 0), stop=(c == DCH - 1))
            q = qT[hp % 2]
            kb.copy(q[:, :], p[:, :TP], e='act')
            for ti in range(3):
                ps_ = pS[is_ % 2]
                is_ += 1
                kb.mm(ps_[:, :128], q[:, ti * 128:(ti + 1) * 128], keysT[:, hp, :])
                kb.copy(S[:, ti, hp, :], ps_[:, :128], e='dve' if ti % 2 else 'act')
        for ti in range(3):
            for h in range(8):
                for side, ta in ((0, t0a), (1, t1a)):
                    s_ = S[:, ti, 2 * h + side, :]
                    kb.op('dve', lambda eng, o=ta, i=s_: eng.max(out=o[:, 0:8], in_=i), r=[S], w=[ta])
                    kb.op('dve', lambda eng, o=tmp128, m=ta, i=s_: eng.match_replace(out=o[:, :], in_to_replace=m[:, 0:8], in_values=i, imm_value=-1e30),
                          r=[S, ta], w=[tmp128])
                    kb.op('dve', lambda eng, o=ta, i=tmp128: eng.max(out=o[:, 8:16], in_=i[:, :]), r=[tmp128], w=[ta])
                kb.tt(cand[:, :, :], t0a[:, :].unsqueeze(2).to_broadcast([128, 16, 16]), t1a[:, :].unsqueeze(1).to_broadcast([128, 16, 16]), ALU.add)
                cf = cand[:, :, :].rearrange("p a b -> p (a b)")
                kb.op('dve', lambda eng, o=best, i=cf: eng.max(out=o[:, 0:8], in_=i), r=[cand], w=[best])
                kb.op('dve', lambda eng, o=cand2, m=best, i=cf: eng.match_replace(out=o[:, :], in_to_replace=m[:, 0:8], in_values=i, imm_value=-1e30), r=[cand, best], w=[cand2])
                kb.op('dve', lambda eng, o=best, i=cand2: eng.max(out=o[:, 8:16], in_=i[:, :]), r=[cand2], w=[best])
                kb.op('dve', lambda eng, o=thr[:, ti, h:h + 1], i=best: eng.tensor_reduce(out=o, in_=i[:, 0:16], op=ALU.min, axis=mybir.AxisListType.X), r=[best], w=[thr])
                kb.ts(nmx[:, :], best[:, 0:1], -1.0, ALU.mult)
                kb.act(eb[:, :], best[:, 0:16], AF.Exp, bias=nmx[:, 0:1])
                kb.op('dve', lambda eng, o=zs, i=eb: eng.reduce_sum(o[:, :], i[:, :], axis=mybir.AxisListType.X), r=[eb], w=[zs])
                kb.act(zs[:, :], zs[:, :], AF.Ln)
                kb.tt(zs[:, :], zs[:, :], best[:, 0:1], ALU.add)
                kb.ts(ncc[:, ti, h:h + 1], zs[:, :], -1.0, ALU.mult)
        kb.copy(h2b[:, :, :], h2[:, :, :], e='act')
        for ti in range(3):
            kb.memset(facc[:, ti, :], 0.0)
        def stage1(grp):
            nonlocal gti
            items = [(ti, h) for ti in range(3) for h in range(8)]
            bufs = []
            for k in range(len(items) + 1):
                if k < len(items):
                    ti, h = items[k]
                    s0 = S[:, ti, 2 * h, grp * KG:(grp + 1) * KG].unsqueeze(2).to_broadcast([128, KG, 128])
                    s1 = S[:, ti, 2 * h + 1, :].unsqueeze(1).to_broadcast([128, KG, 128])
                    tg = tgs[gti % 4]
                    eg = egs[gti % 4]
                    gti += 1
                    kb.tt(tg[:, :, :], s0, s1, ALU.add)
                    kb.act(eg[:, :, :], tg[:, :, :], AF.Exp, bias=ncc[:, ti, h:h + 1])
                    bufs.append((tg, eg))
                if k >= 1:
                    ti, h = items[k - 1]
                    tg, eg = bufs[k - 1]
                    gv = G[grp % 3][ti][:, :].rearrange("p (k q) -> p k q", q=128)
                    if h == 0:
                        kb.stt(gv, tg[:, :, :], thr[:, ti, h:h + 1], eg[:, :, :], ALU.is_ge, ALU.mult)
                    else:
                        kb.stt(eg[:, :, :], tg[:, :, :], thr[:, ti, h:h + 1], eg[:, :, :], ALU.is_ge, ALU.mult)
                        kb.tt(gv, gv, eg[:, :, :], ALU.add, e='pool')
            ut = UTt[grp % 3]
            e0 = grp * KG * 128
            kb.dma('pool', ut[:, :, :], UTv[:, :, e0:e0 + KG * 128])

        def stage2(grp):
            nonlocal ia, it_
            ut = UTt[grp % 3]
            vt = Vt[grp % 3]
            kb.dma('pool', vt[:, :, :], Vv[:, grp * KG:(grp + 1) * KG, :])
            gat = GAT[grp % 2]
            for j in range(KG):
                p = pA[ia % 2]
                ia += 1
                for c in range(DCH):
                    kb.mm(p[:, :TP], ut[:, c, j * 128:(j + 1) * 128], h2b[:, c, :], start=(c == 0), stop=(c == DCH - 1))
                ga = gA[j % 2]
                kb.act(ga[:, :], p[:, :TP], AF.Gelu)
                pt = pT[it_ % 2]
                it_ += 1
                for ti in range(3):
                    kb.transpose(pt[:, ti * 128:(ti + 1) * 128], G[grp % 3][ti][:, j * 128:(j + 1) * 128], ident[:, :])
                kb.tt(gat[:, j, :], ga[:, :], pt[:, :TP], ALU.mult)

        def stage3(grp):
            nonlocal if_
            vt = Vt[grp % 3]
            gat = GAT[grp % 2]
            for ti in range(3):
                for dq in range(4):
                    pf = pF[if_ % 2]
                    if_ += 1
                    for j in range(KG):
                        kb.mm(pf[:, :], gat[:, j, ti * 128:(ti + 1) * 128], vt[:, j, dq * 512:(dq + 1) * 512], start=(j == 0), stop=(j == KG - 1))
                    kb.tt(facc[:, ti, dq * 512:(dq + 1) * 512], facc[:, ti, dq * 512:(dq + 1) * 512], pf[:, :], ALU.add)
        for it in range(NGRP + 2):
            if it < NGRP:
                stage1(it)
            if 0 <= it - 1 < NGRP:
                stage2(it - 1)
            if 0 <= it - 2 < NGRP:
                stage3(it - 2)
        kb.dma('sp', xr[:, :, :], x1v[:, :, tk0:tk0 + TP])
        for c in range(DCH):
            pt = pT[it_ % 2]
            it_ += 1
            for ti in range(3):
                kb.transpose(pt[:, ti * 128:(ti + 1) * 128], facc[:, ti, c * 128:(c + 1) * 128], ident[:, :])
            for ti in range(3):
                seg = 0 if (tp * 3 + ti) < ctx_tiles else 1
                kb.stt(xr[:, c, ti * 128:(ti + 1) * 128], pt[:, ti * 128:(ti + 1) * 128], M['g2'][:, c, seg:seg + 1],
                       xr[:, c, ti * 128:(ti + 1) * 128], ALU.mult, ALU.add)
        if final_nw is not None:
            p = pA[ia % 2]
            ia += 1
            sqf = gA[0]
            for c in range(DCH):
                s_ = gA[c % 2]
                kb.act(s_[:, :], xr[:, c, :], AF.Square)
                kb.mm(p[:, :TP], ones[:, :], s_[:, :], start=(c == 0), stop=(c == DCH - 1))
            rs = qT[0]
            kb.ts(rs[:, :], p[:, :TP], 1.0 / D, ALU.mult, EPS, ALU.add)
            kb.act(rs[:, :], rs[:, :], AF.Sqrt)
            kb.recip(rs[:, :], rs[:, :])
            for c in range(DCH):
                kb.stt(xr[:, c, :], xr[:, c, :], final_nw[:, c:c + 1], rs[:, :], ALU.mult, ALU.mult)
        kb.dma('sp', xov[:, :, tk0:tk0 + TP], xr[:, :, :])
    kb.pop()


NCORES = 8
RW_COLS_, NA_OFF, MB_OFF = 1920, 1920, 3456


def _pc(v):
    return np.ascontiguousarray(np.asarray(v, np.float32).reshape(16, 128).T)


def _rw_host(V, l, hh):
    idx = np.concatenate([hh * 256 + np.arange(256), 512 + hh * 256 + np.arange(256), 1024 + hh * 256 + np.arange(256), np.arange(1536, 1920)])
    my = slice(hh * 256, hh * 256 + 256)

    def col2(v):
        return np.ascontiguousarray(v[my].reshape(2, 128).T).astype(np.float32)
    prm = {}
    prm['mu'] = np.ascontiguousarray(V['rw_mu'][l][:, idx].reshape(2, 9, 128).transpose(2, 1, 0))
    prm['w0'] = np.ascontiguousarray(V['rw_w0'][l][:, my].reshape(2, 2, 128).transpose(2, 1, 0))
    prm['a0'] = np.ascontiguousarray(V['rw_a0'][l][:, my].reshape(2, 2, 128).transpose(2, 1, 0))
    prm['kk'] = col2(V['rw_kk'][l])
    prm['ka'] = col2(V['rw_ka'][l])
    prm['rk'] = col2(V['rw_rk'][l].reshape(512))
    prm['lnw'] = col2(V['rw_ln_w'][l])
    prm['lnb'] = col2(V['rw_ln_b'][l])
    prm['w2'] = np.ascontiguousarray(V['rw_w2'][l][:, :, my].reshape(128, 256))
    prm['a2'] = np.ascontiguousarray(V['rw_a2'][l][:, :, my].reshape(128, 256))
    prm['g2'] = np.ascontiguousarray(V['rw_g2'][l][:, my])
    return idx, prm


def _mb_host(V, l, hh):
    fm = np.concatenate([4480 + hh * 512 + np.arange(512), 5504 + hh * 256 + np.arange(256), 6016 + hh * 256 + np.arange(256)])
    tm = np.concatenate([3456 + hh * 512 + np.arange(512), 6528 + hh * 8 + np.arange(8), 6544 + hh * 8 + np.arange(8)])
    ch = fm - 4480
    prm = {}
    prm['cw'] = np.ascontiguousarray(V['mb_conv_w'][l][:, ch].reshape(5, 8, 128).transpose(2, 1, 0))
    prm['cb'] = np.ascontiguousarray(V['mb_conv_b'][l][ch].reshape(8, 128).T)
    prm['dtb'] = np.ascontiguousarray(V['mb_dt_bias'][l][:, hh * 8: hh * 8 + 8].reshape(16))
    prm['alog'] = np.ascontiguousarray(V['mb_a_log'][l][:, hh * 8: hh * 8 + 8].reshape(16))
    prm['dsk'] = np.ascontiguousarray(V['mb_d'][l][hh * 8: hh * 8 + 8])
    prm['nw'] = np.ascontiguousarray(V['mb_norm_w'][l][hh * 512: hh * 512 + 512])
    return fm, tm, prm


NFM, NTM = 2688, 784
_P1_SHAPES = dict(xT=[2048, 2304], nw1=[128, 16], sc1=[128, 16, 2], sh1=[128, 16, 2], w=[2048, NFM + NTM],
                  rw_mu=[128, 9, 2], rw_w0=[128, 2, 2], rw_a0=[128, 2, 2], rw_kk=[128, 2], rw_ka=[128, 2], rw_rk=[128, 2],
                  rw_lnw=[128, 2], rw_lnb=[128, 2], rw_w2=[128, 256], rw_a2=[128, 256], rw_g2=[128, 256],
                  na_bias=[64, 4, 15, 64], na_mask=[64, 64],
                  mb_cw=[128, 8, 5], mb_cb=[128, 8], mb_dtb=[16], mb_alog=[16], mb_dsk=[8], mb_nw=[512],
                  c_bones=[128, 128], c_ident2=[128, 64], c_ident=[128, 128], c_pmT=[128, 128], c_cos=[128, 2048],
                  c_sin=[128, 2048], c_tri=[2, 128, 128], c_nmask=[2, 128, 128])


def build_p1():
    kb = KB()
    d = {k: kb.dram("i_" + k, s, kind="ExternalInput") for k, s in _P1_SHAPES.items()}
    yT_rw = kb.dram("yT_rw", [256, 2304], kind="ExternalOutput")
    ytok = kb.dram("ytok", [2304, 768], kind="ExternalOutput")
    hT_d = kb.dram("hT_s", [2048, 2304], mybir.dt.bfloat16)
    zT_d = kb.dram("zT_s", [NFM, 2304])
    zk_d = kb.dram("ztok_s", [2304, NTM])
    rws_d = kb.dram("rws_s", [NARR, 128, 2304])
    nw_sb = kb.sb("nw_sb", [128, 16])
    sc_sb = kb.sb("sc_sb", [128, 16, 2])
    sh_sb = kb.sb("sh_sb", [128, 16, 2])
    ones = kb.sb("ones", [128, 128])
    bones = kb.sb("bones", [128, 128])
    ident2 = kb.sb("ident2", [128, 64])
    ident = kb.sb("ident", [128, 128])
    kb.dma('sp', nw_sb[:, :], d['nw1'].ap())
    kb.dma('sp', sc_sb[:, :, :], d['sc1'].ap())
    kb.dma('sp', sh_sb[:, :, :], d['sh1'].ap())
    kb.dma('sp', bones[:, :], d['c_bones'].ap())
    kb.dma('sp', ident2[:, :], d['c_ident2'].ap())
    kb.dma('sp', ident[:, :], d['c_ident'].ap())
    kb.memset(ones[:, :], 1.0)
    kb.push()
    norm_mod_T(kb, d['xT'], hT_d, nw_sb, sc_sb, sh_sb, ones, [0] + [1] * 8, "n1")
    kb.pop()
    kb.push()
    in_proj(kb, hT_d, d['w'], NFM, zT_d, NTM, zk_d, 2304, "ip")
    kb.pop()
    kb.push()
    natten(kb, zT_d, 1152, zk_d, 0, d['na_bias'], d['na_mask'], ytok, 0, "na")
    kb.pop()
    kb.push()
    mamba(kb, zT_d, 1664, zk_d, 256, {k[3:]: v for k, v in d.items() if k.startswith("mb_")},
          dict(ident=ident, pmT=d['c_pmT'], cos=d['c_cos'], sin=d['c_sin'], tri=d['c_tri'], nmask=d['c_nmask']), ytok, 256, "mb")
    kb.pop()
    rwkv(kb, zT_d, 0, {k[3:]: v for k, v in d.items() if k.startswith("rw_")}, rws_d, yT_rw, 0, "rw", dict(bones=bones, ident2=ident2))
    return kb.build()


_P2_SHAPES = dict(yT=[2048, 1152], xT=[2048, 1152], wout=[2048, 2048], g1=[128, 16, 2], sh2=[128, 16, 2], sc2=[128, 16, 2],
                  g2=[128, 16, 2], nw2=[128, 16], wq=[2048, 2048], keysT=[128, 16, 128], UT=[2048, 16384], V=[16384, 2048],
                  c_ident=[128, 128], fnw=[128, 16])


def build_p2(final):
    kb = KB()
    d = {k: kb.dram("i_" + k, s, kind="ExternalInput") for k, s in _P2_SHAPES.items()}
    xo = kb.dram("xo", [2048, 1152], kind="ExternalOutput")
    x1T = kb.dram("x1T_s", [2048, 1152])
    h2T = kb.dram("h2T_s", [2048, 1152])
    ones = kb.sb("ones", [128, 128])
    ident = kb.sb("ident", [128, 128])
    fnw = kb.sb("fnw", [128, 16])
    kb.memset(ones[:, :], 1.0)
    kb.dma('sp', ident[:, :], d['c_ident'].ap())
    kb.dma('sp', fnw[:, :], d['fnw'].ap())
    p2(kb, d['yT'], d['xT'], d, dict(ones=ones, ident=ident), xo, x1T, h2T, "p2", final_nw=fnw if final else None)
    return kb.build()


def build_p0():
    kb = KB()
    cT = kb.dram("cT", [128, 16, 5], kind="ExternalInput")
    aw = kb.dram("aw", [4, 2048, 1536], kind="ExternalInput")
    ab = kb.dram("ab", [128, 4, 12], kind="ExternalInput")
    mo = kb.dram("mo", [128, 4, 12, 5], kind="ExternalOutput")
    sc = kb.sb("sc", [128, 16, 5])
    abs_ = kb.sb("abs", [128, 4, 12])
    out = kb.sb("out", [128, 4, 12, 5])
    kb.dma('sp', sc[:, :, :], cT.ap())
    kb.dma('sp', abs_[:, :, :], ab.ap())
    kb.act(sc[:, :, :], sc[:, :, :], AF.Silu)
    wt = [kb.sb("aw%d" % i, [128, 16, 128]) for i in range(3)]
    pp = [kb.ps("pp%d" % i, [128, 8]) for i in range(2)]
    n = 0
    for l in range(4):
        wv = aw.ap()[l].rearrange("(c p) n -> p c n", p=128)
        for cc in range(12):
            w = wt[n % 3]
            p = pp[n % 2]
            kb.dma('sp' if n % 2 else 'act', w[:, :, :], wv[:, :, cc * 128:(cc + 1) * 128])
            for c in range(16):
                kb.mm(p[:, 0:5], w[:, c, :], sc[:, c, :], start=(c == 0), stop=(c == 15))
            kb.ts(out[:, l, cc, :], p[:, 0:5], abs_[:, l, cc:cc + 1], ALU.add)
            n += 1
    kb.dma('sp', mo.ap(), out[:, :, :, :])
    return kb.build()


_NC_CACHE = {}
_NLAYERS = 4
_DBG = {}


def _get(name, fn):
    if name not in _NC_CACHE:
        _NC_CACHE[name] = fn()
    return _NC_CACHE[name]


def kernel(**inp):
    V = {k: np.asarray(v) for k, v in inp.items()}
    cores = list(range(NCORES))
    c_all = np.concatenate([V['c'], V['c_ctx'][None, :]], 0).astype(np.float32)
    cT = np.ascontiguousarray(c_all.T.reshape(16, 128, 5).transpose(1, 0, 2))
    ins = []
    for c in cores:
        cs = slice(c * 1536, (c + 1) * 1536)
        ins.append(dict(cT=cT, aw=np.ascontiguousarray(V['ada_w'][:, :, cs]),
                        ab=np.ascontiguousarray(V['ada_b'][:, cs].reshape(4, 12, 128).transpose(2, 0, 1))))
    res = run_bass_kernel_spmd(_get('p0', build_p0), ins, core_ids=cores)
    mods = np.zeros((4, 12288, 5), np.float32)
    for c in cores:
        mo = res.results[c]['mo']
        mods[:, c * 1536:(c + 1) * 1536, :] = mo.transpose(1, 2, 0, 3).reshape(4, 1536, 5)

    def seg2(l, j, b):
        v = mods[l, j * 2048:(j + 1) * 2048, :]
        return np.ascontiguousarray(np.stack([_pc(v[:, 4]), _pc(v[:, b])], -1))
    consts = mb_consts()
    cst = dict(c_bones=np.kron(np.eye(2), np.ones((64, 64))).astype(np.float32),
               c_ident2=np.concatenate([np.eye(64), np.eye(64)], 0).astype(np.float32),
               c_ident=np.eye(128, dtype=np.float32))
    cst.update({"c_" + k: v for k, v in consts.items()})
    xs = []
    for c in cores:
        b, th = c // 2, c % 2
        x = np.concatenate([V['ctx'][b][th * 128:(th + 1) * 128], V['x'][b][th * 1024:(th + 1) * 1024]], 0)
        xs.append(np.ascontiguousarray(x.T))
    for l in range(_NLAYERS):
        ins = []
        for c in cores:
            b, hh = c // 2, c % 2
            s0, s1 = xs[2 * b], xs[2 * b + 1]
            xT = np.ascontiguousarray(np.concatenate([s0[:, :128], s1[:, :128], s0[:, 128:], s1[:, 128:]], 1))
            idx, rwp = _rw_host(V, l, hh)
            fm, tm, mbp = _mb_host(V, l, hh)
            na_q = NA_OFF + hh * 256 + np.arange(256)
            cols = np.concatenate([idx, na_q, na_q + 512, fm, na_q + 1024, tm])
            g, mask = na_host_tables(V['na_rpb'][l], slice(hh * 4, hh * 4 + 4))
            dct = dict(xT=xT, nw1=_pc(V['norm1_w'][l]), sc1=seg2(l, 1, b), sh1=seg2(l, 0, b),
                       w=np.ascontiguousarray(V['w_in'][l][:, cols]), na_bias=g, na_mask=mask)
            dct.update({"rw_" + k: v for k, v in rwp.items()})
            dct.update({"mb_" + k: v for k, v in mbp.items()})
            dct.update(cst)
            ins.append({"i_" + k: v for k, v in dct.items()})
        res = run_bass_kernel_spmd(_get('p1', build_p1), ins, core_ids=cores)
        UT = np.ascontiguousarray(V['pe_u'][l].T)
        keysT = np.ascontiguousarray(V['pe_keys'][l].reshape(16, 128, 128).transpose(2, 0, 1))
        ins = []
        for c in cores:
            b, th = c // 2, c % 2
            tok = np.concatenate([th * 128 + np.arange(128), 256 + th * 1024 + np.arange(1024)])
            parts = []
            r0, r1 = res.results[2 * b], res.results[2 * b + 1]
            yT = np.concatenate([r0['yT_rw'][:, tok], r1['yT_rw'][:, tok],
                                 r0['ytok'][tok, 0:256].T, r1['ytok'][tok, 0:256].T,
                                 r0['ytok'][tok, 256:768].T, r1['ytok'][tok, 256:768].T], 0)
            dct = dict(yT=np.ascontiguousarray(yT), xT=xs[c], wout=V['w_out'][l], g1=seg2(l, 2, b), sh2=seg2(l, 3, b),
                       sc2=seg2(l, 4, b), g2=seg2(l, 5, b), nw2=_pc(V['norm2_w'][l]), wq=V['pe_wq'][l], keysT=keysT,
                       UT=UT, V=V['pe_v'][l], c_ident=cst['c_ident'], fnw=_pc(V['final_norm_w']))
            ins.append({"i_" + k: v for k, v in dct.items()})
        final = (l == 3)
        res2 = run_bass_kernel_spmd(_get('p2f' if final else 'p2', lambda: build_p2(final)), ins, core_ids=cores)
        xs = [np.ascontiguousarray(res2.results[c]['xo']) for c in cores]
        _DBG['xs%d' % l] = xs
    out = np.zeros((4, 2048, 2048), np.float32)
    for c in cores:
        b, th = c // 2, c % 2
        out[b, th * 1024:(th + 1) * 1024, :] = xs[c][:, 128:].T
    return out
```
